# Optimizing a Trainium2 kernel written in Bass

```python
import math
import jax, jax.numpy as jnp
from jax import lax
import numpy as np

D_MODEL = 1024
BATCH = 32
SEQ = 256
DEPTH = 4
DEC_BATCH = 4
DEC_SEQ = 2048
PAST_LEN = 256

GRID_W = 64
HEAD_DIM = 64
N_HEADS = D_MODEL // HEAD_DIM
N_KV_HEADS = N_HEADS // 4
Q_BLOCK = 128
ROPE_THETA = 10000.0
CONV_W = D_MODEL // 2
CONV_K = 3
SSM_W = D_MODEL // 2
SSM_GROUP_CH = 16
SSM_GROUPS = SSM_W // SSM_GROUP_CH
SSM_STATE = 64
N_DIR = 2
D_FF = -(-8 * D_MODEL // (3 * 256)) * 256
RMS_EPS = 1e-6
IN_SIZES = (CONV_W, CONV_W, CONV_W, N_HEADS * HEAD_DIM, N_KV_HEADS * HEAD_DIM,
            N_KV_HEADS * HEAD_DIM, SSM_W, D_MODEL, D_MODEL, D_MODEL)
IN_COLS = sum(IN_SIZES)
IN_SPLITS = tuple(int(s) for s in np.cumsum(IN_SIZES)[:-1])

kernel_name = 'hybrid_diffusion_parallel_trunk_step'


def rms_norm(x, g):
    xf = x.astype(jnp.float32)
    y = xf * lax.rsqrt(jnp.mean(xf * xf, axis=-1, keepdims=True) + RMS_EPS)
    return (y * g.astype(jnp.float32)).astype(x.dtype)


def rope_axis(x, pos):
    f = x.shape[-1] // 2
    inv = ROPE_THETA ** (-jnp.arange(f, dtype=jnp.float32) / f)
    ang = pos.astype(jnp.float32)[:, None] * inv[None, :]
    cos = jnp.cos(ang)[None, :, None, :]
    sin = jnp.sin(ang)[None, :, None, :]
    xf = x.astype(jnp.float32)
    x1, x2 = xf[..., :f], xf[..., f:]
    return jnp.concatenate([x1 * cos - x2 * sin, x1 * sin + x2 * cos], axis=-1).astype(x.dtype)


def rope_2d(x):
    length = x.shape[1]
    rows = length // GRID_W
    row = jnp.repeat(jnp.arange(rows), GRID_W)
    col = jnp.tile(jnp.arange(GRID_W), rows)
    half = HEAD_DIM // 2
    return jnp.concatenate([rope_axis(x[..., :half], row), rope_axis(x[..., half:], col)], axis=-1)


def depthwise_conv3(x, w, b):
    y = lax.conv_general_dilated(x, w[:, None, :].astype(x.dtype), window_strides=(1,),
                                 padding=((1, 1),), dimension_numbers=('NWC', 'WIO', 'NWC'),
                                 feature_group_count=x.shape[-1])
    return y + b.astype(x.dtype)


def block_attention(q, k, v):
    bsz, lq, _, _ = q.shape
    rep = N_HEADS // N_KV_HEADS
    nb = lq // Q_BLOCK
    qb = (q * (HEAD_DIM ** -0.5)).reshape(bsz, nb, Q_BLOCK, N_KV_HEADS, rep, HEAD_DIM)
    qb = qb.transpose(1, 0, 2, 3, 4, 5)

    def one_block(qi):
        s = jnp.einsum('bqkrd,bskd->bkrqs', qi, k).astype(jnp.float32)
        p = jax.nn.softmax(s, axis=-1).astype(v.dtype)
        return jnp.einsum('bkrqs,bskd->bqkrd', p, v)

    o = lax.map(one_block, qb)
    return o.transpose(1, 0, 2, 3, 4, 5).reshape(bsz, lq, N_HEADS * HEAD_DIM)


def _complex_affine_combine(e1, e2):
    a1r, a1i, b1r, b1i = e1
    a2r, a2i, b2r, b2i = e2
    return (a2r * a1r - a2i * a1i,
            a2r * a1i + a2i * a1r,
            a2r * b1r - a2i * b1i + b2r,
            a2r * b1i + a2i * b1r + b2i)


def ssm_direction(u, h0_re, h0_im, lam_re, lam_im, b_re, b_im, c_re, c_im, log_dt):
    f32 = jnp.float32
    lam_re = lam_re.astype(f32)
    lam_im = lam_im.astype(f32)
    dt = jnp.exp(log_dt.astype(f32))[:, None]
    mag = jnp.exp(lam_re * dt)
    a_re = mag * jnp.cos(lam_im * dt)
    a_im = mag * jnp.sin(lam_im * dt)
    den = lam_re * lam_re + lam_im * lam_im
    n_re = a_re - 1.0
    coef_re = (n_re * lam_re + a_im * lam_im) / den
    coef_im = (a_im * lam_re - n_re * lam_im) / den
    b_re = b_re.astype(f32)
    b_im = b_im.astype(f32)
    bb_re = coef_re[..., None] * b_re - coef_im[..., None] * b_im
    bb_im = coef_re[..., None] * b_im + coef_im[..., None] * b_re
    x_re = jnp.einsum('blgc,gpc->blgp', u, bb_re)
    x_im = jnp.einsum('blgc,gpc->blgp', u, bb_im)
    h0_re = h0_re.astype(f32)
    h0_im = h0_im.astype(f32)
    x_re = x_re.at[:, 0].add(a_re * h0_re - a_im * h0_im)
    x_im = x_im.at[:, 0].add(a_re * h0_im + a_im * h0_re)
    ar = jnp.broadcast_to(a_re, x_re.shape)
    ai = jnp.broadcast_to(a_im, x_im.shape)
    _, _, h_re, h_im = lax.associative_scan(_complex_affine_combine, (ar, ai, x_re, x_im), axis=1)
    y = (jnp.einsum('gcp,blgp->blgc', c_re.astype(f32), h_re)
         - jnp.einsum('gcp,blgp->blgc', c_im.astype(f32), h_im))
    return y, h_re[:, -1], h_im[:, -1]


def _dir_params(p, d):
    return (p['ssm_lambda_re'][d], p['ssm_lambda_im'][d], p['ssm_b_re'][d], p['ssm_b_im'][d],
            p['ssm_c_re'][d], p['ssm_c_im'][d], p['ssm_log_dt'][d])


def ssm_branch(u, h0_re, h0_im, p):
    bsz, length, _ = u.shape
    uf = u.astype(jnp.float32)
    ug = uf.reshape(bsz, length, SSM_GROUPS, SSM_GROUP_CH)
    y_f, hf_re, hf_im = ssm_direction(ug, h0_re[:, 0], h0_im[:, 0], *_dir_params(p, 0))
    y_b, hb_re, hb_im = ssm_direction(ug[:, ::-1], h0_re[:, 1], h0_im[:, 1], *_dir_params(p, 1))
    y = (y_f + y_b[:, ::-1]).reshape(bsz, length, SSM_W) + p['ssm_d'].astype(jnp.float32) * uf
    y = jax.nn.gelu(y)
    y = y * jax.nn.sigmoid(y @ p['w_glu'].astype(jnp.float32) + p['b_glu'].astype(jnp.float32))
    return y.astype(u.dtype), jnp.stack([hf_re, hb_re], axis=1), jnp.stack([hf_im, hb_im], axis=1)


def trunk_layer(x, mod, p, ctx=None):
    bsz, length, _ = x.shape
    shift1, scale1, gate1, shift2, scale2, gate2 = jnp.split(mod, 6, axis=-1)
    h = rms_norm(x, p['norm1_g']) * (1 + scale1) + shift1
    z = h @ p['w_in']
    cb, cc, cx, q, k, v, u, g_conv, g_attn, g_ssm = jnp.split(z, IN_SPLITS, axis=-1)
    y_conv = cb * depthwise_conv3(cc * cx, p['conv_w'], p['conv_b'])
    q = rms_norm(q.reshape(bsz, length, N_HEADS, HEAD_DIM), p['q_norm_g'])
    k = rms_norm(k.reshape(bsz, length, N_KV_HEADS, HEAD_DIM), p['k_norm_g'])
    v = v.reshape(bsz, length, N_KV_HEADS, HEAD_DIM)
    if ctx is None:
        k_att, v_att = k, v
        h0_re = jnp.zeros((bsz, N_DIR, SSM_GROUPS, SSM_STATE), jnp.float32)
        h0_im = jnp.zeros((bsz, N_DIR, SSM_GROUPS, SSM_STATE), jnp.float32)
    else:
        k_ctx, v_ctx, h0_re, h0_im = ctx
        q = rope_2d(q)
        k_att = jnp.concatenate([rope_2d(k), k_ctx.astype(k.dtype)], axis=1)
        v_att = jnp.concatenate([v, v_ctx.astype(v.dtype)], axis=1)
    y_attn = block_attention(q, k_att, v_att)
    y_ssm, hT_re, hT_im = ssm_branch(u, h0_re, h0_im, p)
    merged = (jax.nn.sigmoid(g_conv) * (y_conv @ p['w_proj_conv'])
              + jax.nn.sigmoid(g_attn) * (y_attn @ p['w_proj_attn'])
              + jax.nn.sigmoid(g_ssm) * (y_ssm @ p['w_proj_ssm']))
    x = x + gate1 * (merged @ p['w_out'])
    h = rms_norm(x, p['norm2_g']) * (1 + scale2) + shift2
    x = x + gate2 * ((jax.nn.silu(h @ p['w_ffn_gate']) * (h @ p['w_ffn_up'])) @ p['w_ffn_down'])
    if ctx is None:
        return x, (k, v, hT_re, hT_im)
    return x, None


def setup_inputs(seed: int = 0) -> dict:
    key = jax.random.key(seed)
    keys = iter(jax.random.split(key, 48))
    f32 = jnp.float32

    def nrm(shape, scale):
        return jax.random.normal(next(keys), shape, f32) * scale

    ssm_shape = (DEPTH, N_DIR, SSM_GROUPS, SSM_STATE)
    n_idx = jnp.arange(SSM_STATE, dtype=f32)
    return {
        'x_prompt': nrm((BATCH, SEQ, D_MODEL), 1.0),
        'x_sample': nrm((DEC_BATCH, DEC_SEQ, D_MODEL), 1.0),
        'c': nrm((DEC_BATCH, D_MODEL), 1.0),
        'cache_k': nrm((DEC_BATCH, DEPTH, PAST_LEN, N_KV_HEADS, HEAD_DIM), 1.0),
        'cache_v': nrm((DEC_BATCH, DEPTH, PAST_LEN, N_KV_HEADS, HEAD_DIM), 1.0),
        'state_ssm_re': nrm((DEC_BATCH, DEPTH, N_DIR, SSM_GROUPS, SSM_STATE), 0.5),
        'state_ssm_im': nrm((DEC_BATCH, DEPTH, N_DIR, SSM_GROUPS, SSM_STATE), 0.5),
        'c_ctx': nrm((D_MODEL,), 1.0),
        'w_ada': nrm((DEPTH, D_MODEL, 6 * D_MODEL), D_MODEL ** -0.5),
        'b_ada': nrm((DEPTH, 6 * D_MODEL), 0.02),
        'norm1_g': 1.0 + nrm((DEPTH, D_MODEL), 0.02),
        'w_in': nrm((DEPTH, D_MODEL, IN_COLS), D_MODEL ** -0.5),
        'conv_w': nrm((DEPTH, CONV_K, CONV_W), CONV_K ** -0.5),
        'conv_b': nrm((DEPTH, CONV_W), 0.02),
        'q_norm_g': 1.0 + nrm((DEPTH, HEAD_DIM), 0.02),
        'k_norm_g': 1.0 + nrm((DEPTH, HEAD_DIM), 0.02),
        'ssm_lambda_re': -0.5 + nrm(ssm_shape, 0.01),
        'ssm_lambda_im': math.pi * n_idx + nrm(ssm_shape, 0.01),
        'ssm_b_re': nrm(ssm_shape + (SSM_GROUP_CH,), (2 * SSM_GROUP_CH) ** -0.5),
        'ssm_b_im': nrm(ssm_shape + (SSM_GROUP_CH,), (2 * SSM_GROUP_CH) ** -0.5),
        'ssm_c_re': nrm((DEPTH, N_DIR, SSM_GROUPS, SSM_GROUP_CH, SSM_STATE), (2 * SSM_STATE) ** -0.5),
        'ssm_c_im': nrm((DEPTH, N_DIR, SSM_GROUPS, SSM_GROUP_CH, SSM_STATE), (2 * SSM_STATE) ** -0.5),
        'ssm_log_dt': jax.random.uniform(next(keys), (DEPTH, N_DIR, SSM_GROUPS), f32,
                                         math.log(1e-3), math.log(1e-1)),
        'ssm_d': nrm((DEPTH, SSM_W), 1.0),
        'w_glu': nrm((DEPTH, SSM_W, SSM_W), SSM_W ** -0.5),
        'b_glu': nrm((DEPTH, SSM_W), 0.02),
        'w_proj_conv': nrm((DEPTH, CONV_W, D_MODEL), CONV_W ** -0.5),
        'w_proj_attn': nrm((DEPTH, N_HEADS * HEAD_DIM, D_MODEL), (N_HEADS * HEAD_DIM) ** -0.5),
        'w_proj_ssm': nrm((DEPTH, SSM_W, D_MODEL), SSM_W ** -0.5),
        'w_out': nrm((DEPTH, D_MODEL, D_MODEL), D_MODEL ** -0.5),
        'norm2_g': 1.0 + nrm((DEPTH, D_MODEL), 0.02),
        'w_ffn_gate': nrm((DEPTH, D_MODEL, D_FF), D_MODEL ** -0.5),
        'w_ffn_up': nrm((DEPTH, D_MODEL, D_FF), D_MODEL ** -0.5),
        'w_ffn_down': nrm((DEPTH, D_FF, D_MODEL), D_FF ** -0.5),
        'final_norm_g': 1.0 + nrm((D_MODEL,), 0.02),
    }


def reference(x_prompt, x_sample, c, cache_k, cache_v, state_ssm_re, state_ssm_im, c_ctx,
              w_ada, b_ada, norm1_g, w_in, conv_w, conv_b, q_norm_g, k_norm_g,
              ssm_lambda_re, ssm_lambda_im, ssm_b_re, ssm_b_im, ssm_c_re, ssm_c_im,
              ssm_log_dt, ssm_d, w_glu, b_glu, w_proj_conv, w_proj_attn, w_proj_ssm,
              w_out, norm2_g, w_ffn_gate, w_ffn_up, w_ffn_down, final_norm_g):
    def layer_params(l):
        return {
            'norm1_g': norm1_g[l], 'w_in': w_in[l], 'conv_w': conv_w[l], 'conv_b': conv_b[l],
            'q_norm_g': q_norm_g[l], 'k_norm_g': k_norm_g[l],
            'ssm_lambda_re': ssm_lambda_re[l], 'ssm_lambda_im': ssm_lambda_im[l],
            'ssm_b_re': ssm_b_re[l], 'ssm_b_im': ssm_b_im[l],
            'ssm_c_re': ssm_c_re[l], 'ssm_c_im': ssm_c_im[l],
            'ssm_log_dt': ssm_log_dt[l], 'ssm_d': ssm_d[l], 'w_glu': w_glu[l], 'b_glu': b_glu[l],
            'w_proj_conv': w_proj_conv[l], 'w_proj_attn': w_proj_attn[l],
            'w_proj_ssm': w_proj_ssm[l], 'w_out': w_out[l], 'norm2_g': norm2_g[l],
            'w_ffn_gate': w_ffn_gate[l], 'w_ffn_up': w_ffn_up[l], 'w_ffn_down': w_ffn_down[l],
        }

    xp = x_prompt
    ks, vs, srs, sis = [], [], [], []
    for l in range(DEPTH):
        mod_ctx = (jax.nn.silu(c_ctx) @ w_ada[l] + b_ada[l])[None, None, :]
        xp, (k_l, v_l, sr_l, si_l) = trunk_layer(xp, mod_ctx, layer_params(l), None)
        ks.append(k_l)
        vs.append(v_l)
        srs.append(sr_l)
        sis.append(si_l)
    y_prompt = rms_norm(xp, final_norm_g)
    new_cache_k = jnp.stack(ks, axis=1)
    new_cache_v = jnp.stack(vs, axis=1)
    new_state_ssm_re = jnp.stack(srs, axis=1)
    new_state_ssm_im = jnp.stack(sis, axis=1)

    xs = x_sample
    for l in range(DEPTH):
        mod_lat = (jax.nn.silu(c) @ w_ada[l] + b_ada[l])[:, None, :]
        ctx = (cache_k[:, l], cache_v[:, l], state_ssm_re[:, l], state_ssm_im[:, l])
        xs, _ = trunk_layer(xs, mod_lat, layer_params(l), ctx)
    y_sample = rms_norm(xs, final_norm_g)

    return (y_prompt, y_sample, new_cache_k, new_cache_v, new_state_ssm_re, new_state_ssm_im)
```

```python
import math
import numpy as np
import concourse.bass as bass
import concourse.mybir as mybir
from concourse.bass_utils import run_bass_kernel_spmd
from contextlib import ExitStack

F32 = mybir.dt.float32
BF16 = mybir.dt.bfloat16
AF = mybir.ActivationFunctionType
ALU = mybir.AluOpType

D = 1024
T = 2048
DEPTH = 4
NT = 8
CH = 512
NCH = T // CH
INC = 6656
DFF = 2816
NFT = DFF // 128
EPS = 1e-6
BIG = 1024.0
NK = 18
NCK = 256
SAME_SYNC = True
ENABLE_SSM = True
ENABLE_ATTN = True
EPOCH = 12000
_STATS = {}
STAGE = 6
SSM_SUB = 6


_DT_SIZE = {str(F32): 4, str(BF16): 2}


def _region(ap):
    name = ap.tensor.name
    esz = _DT_SIZE[str(ap.dtype)]
    pairs = [(int(s_), int(c_)) for s_, c_ in ap.ap]
    off = int(ap.offset)
    if type(ap.tensor).__name__.startswith('DRam'):
        span = sum((c_ - 1) * abs(s_) for s_, c_ in pairs)
        return (name, 0, 1, off * esz, (off + span + 1) * esz)
    pstride, pcount = pairs[0]
    if type(ap.tensor).__name__.startswith('PSum'):
        return (name, 0, 128, 0, 2048)
    if pstride == 0:
        pstride = int(ap.tensor.shape[-1]) if len(ap.tensor.shape) == 2 else 1
        for d_ in ap.tensor.shape[1:]:
            pass
    p0 = off // pstride if pstride else 0
    f0 = off - p0 * pstride
    span = sum((c_ - 1) * abs(s_) for s_, c_ in pairs[1:])
    return (name, p0, p0 + pcount, f0 * esz, (f0 + span + 1) * esz)


def _overlap(a, b):
    return a[1] < b[2] and b[1] < a[2] and a[3] < b[4] and b[3] < a[4]


def _covers(a, b):
    return a[1] <= b[1] and a[2] >= b[2] and a[3] <= b[3] and a[4] >= b[4]


class Sched:
    def __init__(self, nc):
        self.nc = nc
        self.ops = []
        self.recs = {}
        self.dma_count = {}

    def add(self, eng, fn, reads=(), writes=(), dma=False, dkey=None, rd_aps=(), wr_aps=()):
        idx = len(self.ops)
        raw, oth, dwaits = set(), set(), {}

        def dep(p, kind):
            if p is None or p == idx:
                return
            po = self.ops[p]
            if po['dma']:
                dwaits[po['dkey']] = self.dma_count[po['dkey']] * 16
            elif kind == 'raw':
                raw.add(p)
            else:
                oth.add(p)
        rregs = [_region(a) for a in rd_aps]
        wregs = [_region(a) for a in wr_aps]
        for R in rregs:
            for rec in self.recs.get(R[0], ()):
                if rec[1] is not None and _overlap(rec[0], R):
                    dep(rec[1], 'raw')
                if R[0].startswith('ps') and _overlap(rec[0], R):
                    for e2, rd in rec[2].items():
                        if e2 != eng:
                            dep(rd, 'war')
        for R in wregs:
            for rec in self.recs.get(R[0], ()):
                if _overlap(rec[0], R):
                    dep(rec[1], 'waw')
                    for rd in rec[2].values():
                        dep(rd, 'war')
        op = dict(eng=eng, fn=fn, raw=raw, oth=oth, dwaits=dwaits, dma=dma, dkey=dkey, signal=False)
        if dma:
            self.dma_count[dkey] = self.dma_count.get(dkey, 0) + 1
        self.ops.append(op)
        for R in rregs:
            lst = self.recs.setdefault(R[0], [])
            for rec in lst:
                if rec[0] == R:
                    rec[2][eng] = idx
                    break
            else:
                lst.append([R, None, {eng: idx}])
        for R in wregs:
            lst = self.recs.setdefault(R[0], [])
            lst[:] = [rec for rec in lst if not _covers(R, rec[0])]
            lst.append([R, idx, {}])
        return idx

    def emit(self, es, final_store_keys):
        nc = self.nc
        for op in self.ops:
            for p in op['raw']:
                po = self.ops[p]
                if po['eng'] != op['eng'] or (SAME_SYNC and op['eng'] in ('act', 'dve', 'pool')):
                    po['signal'] = True
            for p in op['oth']:
                po = self.ops[p]
                if po['eng'] != op['eng'] or (SAME_SYNC and op['eng'] in ('act', 'dve', 'pool')):
                    po['signal'] = True
        cnt = {}
        for op in self.ops:
            if op['signal']:
                c = cnt.get(op['eng'], 0)
                op['ticket'] = (c // EPOCH, c % EPOCH + 1)
                cnt[op['eng']] = c + 1
        sems = {}

        def sem(key):
            if key not in sems:
                sems[key] = es.enter_context(nc.semaphore("s%d" % len(sems)))
            return sems[key]
        for e, c in cnt.items():
            for ep in range(c // EPOCH + 1):
                sem(('e', e, ep))
        for k in self.dma_count:
            sem(('d', k))
        block = es.enter_context(nc.Block())
        engmap = [('pe', block.tensor), ('act', block.scalar), ('dve', block.vector),
                  ('pool', block.gpsimd), ('sp', block.sync)]
        ops = self.ops
        dma_count = self.dma_count
        for ename, deco in engmap:
            ops_e = [op for op in ops if op['eng'] == ename]

            def body(eng, ops_e=ops_e, ename=ename):
                waited = {}
                for op in ops_e:
                    waits = {}
                    for p in list(op['raw']) + list(op['oth']):
                        po = ops[p]
                        if not po['signal']:
                            continue
                        if po['eng'] == ename and not (SAME_SYNC and ename in ('act', 'dve', 'pool')):
                            continue
                        k = ('e', po['eng'], po['ticket'][0])
                        waits[k] = max(waits.get(k, 0), po['ticket'][1])
                    for dk, v in op['dwaits'].items():
                        k = ('d', dk)
                        waits[k] = max(waits.get(k, 0), v)
                    for k, v in waits.items():
                        if waited.get(k, 0) >= v:
                            continue
                        if k[0] == 'e':
                            newer = [kk for kk in waited if kk[0] == 'e' and kk[1] == k[1] and kk[2] > k[2]]
                            if newer:
                                continue
                        eng.wait_ge(sems[k], v)
                        waited[k] = v
                    ins = op['fn'](eng)
                    if op['dma']:
                        ins.then_inc(sems[('d', op['dkey'])], 16)
                    elif op['signal']:
                        ins.then_inc(sems[('e', ename, op['ticket'][0])], 1)
                if ename == 'sp':
                    for dk in final_store_keys:
                        eng.wait_ge(sems[('d', dk)], dma_count[dk] * 16)
            deco(body)


def build_program():
    nc = bass.Bass("TRN2", target_bir_lowering=False)
    S = Sched(nc)
    es = ExitStack()

    def din(name, shape):
        return nc.dram_tensor(name, list(shape), F32, kind="ExternalInput").ap()

    def dout(name, shape):
        return nc.dram_tensor(name, list(shape), F32, kind="ExternalOutput").ap()

    xT_d = din("xT", [D, T])
    cvec_d = din("cvec", [128, NT])
    freq_d = din("freq", [64, 1])
    role_d = din("role", [128, 1])
    qmask_d = din("qmask", [10, T])
    kmask_d = din("kmask", [10, NK * 128])
    ckT_d = din("ckT", [DEPTH, 4, 64, 256])
    cv_d = din("cv", [DEPTH, 128, 2, 256])
    h0_d = din("h0", [DEPTH, 4, 128, 16])
    ident_d = din("ident", [128, 128])
    jmat_d = din("jmat", [128, 128])
    rot_d = din("rot", [64, 64])
    amask_d = din("amask", [128, 2, 128])
    w_ada_d = din("w_ada", [DEPTH, D, 6 * D])
    b_ada_d = din("b_ada", [DEPTH, 128, 48])
    n1g_d = din("n1g", [DEPTH, 128, NT])
    n2g_d = din("n2g", [DEPTH, 128, NT])
    fng_d = din("fng", [128, NT])
    w_in_d = din("w_in", [DEPTH, D, INC])
    convw_d = din("convw", [DEPTH, 128, 4, 4])
    qkg_d = din("qkg", [DEPTH, 64, 2])
    lam_d = din("lam", [DEPTH, 4, 128, 3, 8])
    sB_d = din("sB", [DEPTH, 4, 128, 2, 128])
    sC_d = din("sC", [DEPTH, 4, 128, 2, 128])
    sD_d = din("sD", [DEPTH, 128, 32])
    w_glu_d = din("w_glu", [DEPTH, 512, 512])
    b_glu_d = din("b_glu", [DEPTH, 128, 4])
    wpc_d = din("wpc", [DEPTH, 512, D])
    wpa_d = din("wpa", [DEPTH, D, D])
    wps_d = din("wps", [DEPTH, 512, D])
    w_out_d = din("w_out", [DEPTH, D, D])
    wfg_d = din("wfg", [DEPTH, D, DFF])
    wfu_d = din("wfu", [DEPTH, D, DFF])
    wfd_d = din("wfd", [DEPTH, DFF, D])

    yT_d = dout("yT", [D, T])
    kout_d = dout("kout", [DEPTH, 4, 64, T])
    vout_d = dout("vout", [DEPTH, 4, 128, 16 * 64])
    sout_d = dout("sout", [DEPTH, 4, 128, 128])

    def sb(name, shape, dt=F32):
        return es.enter_context(nc.sbuf_tensor("sb_" + name, list(shape), dt))

    xT = sb("xT_sb", [128, NT, T])
    hT = sb("hT_sb", [128, NT, T], BF16)
    merged = sb("merged", [128, NT * T], BF16)
    wA = sb("wA", [128, 4096], BF16)
    wB = sb("wB", [128, 4096], BF16)
    SCR_N = 20 * 1024
    scr = sb("scr", [128, SCR_N], BF16)
    ident = sb("ident", [128, 128])
    identb = sb("identb", [128, 128], BF16)
    jmatb = sb("jmatb", [128, 128], BF16)
    ones_f = sb("ones_f", [128, 128])
    onesb = sb("onesb", [64, 64], BF16)
    sqb = sb("sqb", [64, 512], BF16)
    rotm = sb("rotm", [64, 64])
    amask = sb("amask", [128, 2, 128])
    modv = sb("modv", [128, DEPTH, 48])
    badat = sb("badat", [128, DEPTH, 48])
    A1 = sb("A1", [128, DEPTH, NT])
    A2 = sb("A2", [128, DEPTH, NT])
    n1g = sb("n1g", [128, DEPTH, NT])
    n2g = sb("n2g", [128, DEPTH, NT])
    fng = sb("fng", [128, NT])
    cvec = sb("cvec", [128, NT])
    scb = sb("scb", [128, NT], BF16)
    epst = sb("epst", [128, 1])
    halfpi = sb("halfpi", [128, 1])
    role = sb("role", [128, 1])
    rolem1 = sb("rolem1", [128, 1])
    freq = sb("freq", [64, 1])
    pw = sb("pw", [64, 2, 8])
    tabC = sb("tabC", [64, 64])
    tabS = sb("tabS", [64, 64])
    ttmp = sb("ttmp", [64, 64])
    convw = sb("convw", [128, DEPTH, 4, 4])
    convx = sb("convx", [128, 4, 2])
    qkg = sb("qkg", [64, DEPTH, 2])
    bglu = sb("bglu", [128, DEPTH, 4])
    sDt = sb("sDt", [128, DEPTH, 32])
    sq_t = [sb("sq%d" % i, [128, 512]) for i in range(2)]
    rstd = sb("rstd", [128, 512])
    tmpf = [sb("tmpf%d" % i, [128, 512]) for i in range(3)]
    tmpb = [sb("tmpb%d" % i, [128, 512], BF16) for i in range(4)]
    ps = [es.enter_context(nc.psum_tensor("ps%d" % i, [128, 512], F32)) for i in range(8)]

    def _aps(*xs):
        return [x for x in xs if isinstance(x, bass.AP)]

    def mm(out, lhsT, rhs, start, stop, reads=(), writes=(), **kw):
        S.add('pe', lambda e: e.matmul(out, lhsT=lhsT, rhs=rhs, start=start, stop=stop, **kw),
              rd_aps=_aps(lhsT, rhs), wr_aps=_aps(out))

    def act(out, in_, func, reads=(), writes=(), **kw):
        S.add('act', lambda e: e.activation(out=out, in_=in_, func=func, **kw),
              rd_aps=_aps(in_, kw.get('bias'), kw.get('scale')), wr_aps=_aps(out))

    def tt(out, in0, in1, op, reads=(), writes=(), eng='dve'):
        S.add(eng, lambda e: e.tensor_tensor(out=out, in0=in0, in1=in1, op=op), rd_aps=_aps(in0, in1), wr_aps=_aps(out))

    def ts(out, in0, s1, op0, reads=(), writes=(), s2=None, op1=None, eng='dve'):
        if op1 is None:
            S.add(eng, lambda e: e.tensor_scalar(out=out, in0=in0, scalar1=s1, scalar2=None, op0=op0),
                  rd_aps=_aps(in0, s1), wr_aps=_aps(out))
        else:
            S.add(eng, lambda e: e.tensor_scalar(out=out, in0=in0, scalar1=s1, scalar2=s2, op0=op0, op1=op1),
                  rd_aps=_aps(in0, s1, s2), wr_aps=_aps(out))

    def stt(out, in0, scalar, in1, op0, op1, reads=(), writes=()):
        S.add('dve', lambda e: e.scalar_tensor_tensor(out=out, in0=in0, scalar=scalar, in1=in1, op0=op0, op1=op1),
              rd_aps=_aps(in0, scalar, in1), wr_aps=_aps(out))

    def cp(out, in_, reads=(), writes=(), eng='dve'):
        if eng == 'act':
            S.add('act', lambda e: e.activation(out=out, in_=in_, func=AF.Copy), rd_aps=_aps(in_), wr_aps=_aps(out))
        else:
            S.add(eng, lambda e: e.tensor_copy(out=out, in_=in_), rd_aps=_aps(in_), wr_aps=_aps(out))

    def recip(out, in_, reads=(), writes=()):
        S.add('dve', lambda e: e.reciprocal(out=out, in_=in_), rd_aps=_aps(in_), wr_aps=_aps(out))

    def memset(ap, val, writes=(), eng='dve'):
        S.add(eng, lambda e: e.memset(ap, val), wr_aps=_aps(ap))

    def dma(q, out, in_, reads, writes, dkey):
        S.add(q, lambda e: e.dma_start(out=out, in_=in_), dma=True, dkey=dkey, rd_aps=_aps(in_), wr_aps=_aps(out))

    store_keys = []

    def store(out, in_, reads, key):
        dk = ('st', key)
        if dk not in store_keys:
            store_keys.append(dk)
        dma('sp', out, in_, reads, (), dk)

    wstate = {'A': 0, 'B': 0}

    def wload(pool, dram_ap, full=False):
        tile_ = wA if pool == 'A' else wB
        kt_, ncol = dram_ap.shape[1], dram_ap.shape[2]
        n = kt_ * ncol
        if full:
            assert n <= 4096
            keys = (('w', pool, 0), ('w', pool, 1))
            view = tile_[:, 0:n]
        else:
            assert n <= 2048
            i = wstate[pool] % 2
            wstate[pool] += 1
            keys = (('w', pool, i),)
            view = tile_[:, i * 2048:i * 2048 + n]
        view = view.rearrange("p (a b) -> p a b", b=ncol)
        dma('pool', view, dram_ap, (), keys, keys[0])
        return view, keys[0]

    def wblock(w_ap, c0, ncols):
        return w_ap[:, c0:c0 + ncols].rearrange("(kt p) c -> p kt c", p=128)

    def simple_load(tile_ap, dram_ap, key, q='sp'):
        dma(q, tile_ap, dram_ap, (), (key,), key)

    psrot = {'i': 0}

    def next_ps(lo, n):
        i = lo + (psrot['i'] % n)
        psrot['i'] += 1
        return i

    def scr_view(off, n, dt=BF16):
        assert off + n <= SCR_N, (off, n)
        v = scr[:, off:off + n]
        if dt == F32:
            v = v.bitcast(F32)
        return v

    def mg_view(off, n, dt=BF16):
        assert off + n <= NT * T
        v = merged[:, off:off + n]
        if dt == F32:
            v = v.bitcast(F32)
        return v

    mergedv = merged[:, :].rearrange("p (a t) -> p a t", t=T)

    for kt in range(NT):
        simple_load(xT[:, kt, :], xT_d[kt * 128:(kt + 1) * 128, :], ('x', kt))
    simple_load(ident[:], ident_d, 'ident')
    dma('pool', jmatb[:], jmat_d, (), (), 'jmatb')
    simple_load(rotm[:], rot_d, 'rotm')
    simple_load(amask[:], amask_d, 'amask')
    simple_load(cvec[:], cvec_d, 'cvec')
    simple_load(freq[:], freq_d, 'freq')
    simple_load(role[:], role_d, 'role')
    simple_load(fng[:], fng_d, 'fng')
    for l in range(DEPTH):
        simple_load(badat[:, l, :], b_ada_d[l], 'badat')
        simple_load(n1g[:, l, :], n1g_d[l], 'n1g')
        simple_load(n2g[:, l, :], n2g_d[l], 'n2g')
        simple_load(convw[:, l], convw_d[l], 'convw')
        simple_load(qkg[:, l, :], qkg_d[l], 'qkg')
        simple_load(bglu[:, l, :], b_glu_d[l], 'bglu')
        simple_load(sDt[:, l, :], sD_d[l], 'sDt')
    memset(ones_f[:], 1.0, ('ones_f',))
    memset(onesb[:], 1.0)
    memset(epst[:], EPS, ('epst',))
    memset(halfpi[:], math.pi / 2, ('halfpi',))
    cp(identb[:], ident[:], ('ident',), ('identb',))
    ts(rolem1[:], role[:], -1.0, ALU.add, ('role',), ('rolem1',))

    act(scb[:], cvec[:], AF.Silu, ('cvec',), ('scb',))

    def compute_mods_gen(l):
        for cb in range(24):
            wv, wk = wload('A', wblock(w_ada_d[l], cb * 256, 256))
            pb = next_ps(0, 3)
            for t_ in range(2):
                for kt in range(NT):
                    mm(ps[pb][:, t_:t_ + 1], wv[:, kt, t_ * 128:(t_ + 1) * 128], scb[:, kt:kt + 1],
                       kt == 0, kt == NT - 1)
            tt(modv[:, l, 2 * cb:2 * cb + 2], ps[pb][:, 0:2], badat[:, l, 2 * cb:2 * cb + 2], ALU.add)
            yield
        stt(A1[:, l, :], modv[:, l, 8:16], 1.0, n1g[:, l, :], ALU.add, ALU.mult)
        stt(A2[:, l, :], modv[:, l, 32:40], 1.0, n2g[:, l, :], ALU.add, ALU.mult)

    def compute_mods(l):
        for _ in compute_mods_gen(l):
            pass
    compute_mods(0)

    act(pw[:, 0, 0:1], freq[:], AF.Sin, ('freq', 'halfpi'), ('pw',), bias=halfpi[0:64, :])
    act(pw[:, 1, 0:1], freq[:], AF.Sin, ('freq', 'pw'), ('pw',))
    for m in range(6):
        c_m, s_m = pw[:, 0, m:m + 1], pw[:, 1, m:m + 1]
        tt(ttmp[:, 0:1], s_m, s_m, ALU.mult, ('pw',), ('ttmp',))
        stt(pw[:, 0, m + 1:m + 2], c_m, c_m, ttmp[:, 0:1], ALU.mult, ALU.subtract, ('pw', 'ttmp'), ('pw',))
        stt(pw[:, 1, m + 1:m + 2], c_m, 2.0, s_m, ALU.mult, ALU.mult, ('pw',), ('pw',))
    memset(tabC[:, 0:1], 1.0, ('tab',))
    memset(tabS[:, 0:1], 0.0, ('tab',))
    for m in range(6):
        n = 1 << m
        c_m, s_m = pw[:, 0, m:m + 1], pw[:, 1, m:m + 1]
        ts(ttmp[:, 0:n], tabC[:, 0:n], s_m, ALU.mult, ('tab', 'pw'), ('ttmp',))
        stt(tabS[:, n:2 * n], tabS[:, 0:n], c_m, ttmp[:, 0:n], ALU.mult, ALU.add, ('tab', 'ttmp', 'pw'), ('tabS2',))
        ts(ttmp[:, 0:n], tabS[:, 0:n], s_m, ALU.mult, ('tab', 'tabS2', 'pw'), ('ttmp',))
        stt(tabC[:, n:2 * n], tabC[:, 0:n], c_m, ttmp[:, 0:n], ALU.mult, ALU.subtract, ('tab', 'ttmp', 'pw', 'tabS2'), ('tab',))

    def rms_stats(c):
        cs = slice(c * CH, (c + 1) * CH)
        for kt in range(NT):
            sq = sq_t[kt % 2]
            act(sq[:], xT[:, kt, cs], AF.Square, (('x', kt),), (('sq', kt % 2),))
            mm(ps[7][:], ones_f[:], sq[:], kt == 0, kt == NT - 1, (('sq', kt % 2), 'ones_f'), (('ps', 7),))
        act(rstd[:], ps[7][:], AF.Ln, scale=1.0 / D, bias=epst[:])
        act(rstd[:], rstd[:], AF.Exp, scale=-0.5)

    def rmsnorm_to_hT(Acoef, Bshift, l):
        for c in range(NCH):
            cs = slice(c * CH, (c + 1) * CH)
            rms_stats(c)
            for kt in range(NT):
                tf = tmpf[kt % 2]
                stt(tf[:], xT[:, kt, cs], Acoef(kt), rstd[:], ALU.mult, ALU.mult,
                    (('x', kt), 'rstd', ('A1', l), ('A2', l)), (('tmpf', kt % 2),))
                act(hT[:, kt, cs], tf[:], AF.Identity, (('tmpf', kt % 2), ('modv', l)), (('hT', kt, c),),
                    bias=Bshift(kt))

    def proj_hT(wv, wk, col0, M, c, out_ps, pskey):
        cs = slice(c * CH, (c + 1) * CH)
        for kt in range(NT):
            mm(out_ps, wv[:, kt, col0:col0 + M], hT[:, kt, cs], kt == 0, kt == NT - 1,
               (wk, ('hT', kt, c)), (pskey,))

    def merge_contrib(l, first, ysrc, ykeys, nkt, wp_d, gate_col0, chunks):
        pcols = 256 if nkt == 8 else 512
        for nb in range(D // 256):
            gv, gk = wload('A', wblock(w_in_d[l], gate_col0 + nb * 256, 256))
            if (nb * 256) % pcols == 0:
                pv, pk = wload('B', wblock(wp_d[l], nb * 256, pcols))
            poff = (nb * 256) % pcols
            for t_ in range(2):
                dt_ = nb * 2 + t_
                for c in chunks:
                    cs = slice(c * CH, (c + 1) * CH)
                    pg = next_ps(0, 3)
                    proj_hT(gv, gk, t_ * 128, 128, c, ps[pg][:], ('ps', pg))
                    pp = next_ps(3, 3)
                    for kt in range(nkt):
                        mm(ps[pp][:], pv[:, kt, poff + t_ * 128:poff + (t_ + 1) * 128], ysrc(kt, c), kt == 0, kt == nkt - 1,
                           (pk, ykeys(kt, c)), (('ps', pp),))
                    sg = tmpf[2]
                    act(sg[:], ps[pg][:], AF.Sigmoid, (('ps', pg),), (('tmpf', 2),))
                    if first:
                        tt(mergedv[:, dt_, cs], ps[pp][:], sg[:], ALU.mult, (('ps', pp), ('tmpf', 2)), (('mg', dt_, c),))
                    else:
                        tb = tmpb[0]
                        tt(tb[:], ps[pp][:], sg[:], ALU.mult, (('ps', pp), ('tmpf', 2)), (('tmpb', 0),))
                        tt(mergedv[:, dt_, cs], mergedv[:, dt_, cs], tb[:], ALU.add, (('tmpb', 0), ('mg', dt_, c)),
                           (('mg', dt_, c),))

    def head_norm_rope_gen(pq, pqkey, gain, cglob, out_bf, out_keys, kstore=None):
        sq = sqb
        act(sq[0:64, :], pq, AF.Square, (pqkey,), (('sq', 0),))
        yield
        mm(ps[7][0:64, :], onesb[0:64, 0:64], sq[0:64, :], True, True, (('sq', 0), 'ones_f'), (('ps', 7),))
        act(rstd[0:64, :], ps[7][0:64, :], AF.Ln, scale=1.0 / 64, bias=epst[0:64, :])
        act(rstd[0:64, :], rstd[0:64, :], AF.Exp, scale=-0.5)
        qn = tmpf[0]
        stt(qn[0:64, :], pq, gain, rstd[0:64, :], ALU.mult, ALU.mult, (pqkey, 'rstd', 'qkg'), (('tmpf', 0),))
        if kstore is not None:
            kstore(qn)
        yield
        mm(ps[6][0:64, :], rotm[:, :], qn[0:64, :], True, True, (('tmpf', 0), 'rotm'), (('ps', 6),))
        r0 = cglob * 8
        t1 = tmpf[1]
        for (p0, tab_c, tab_s) in ((0, tabC, tabS), (32, tabC, tabS)):
            if p0 == 0:
                cb_ = tab_c[0:32, r0:r0 + 8].unsqueeze(2).to_broadcast([32, 8, 64])
                sb_ = tab_s[0:32, r0:r0 + 8].unsqueeze(2).to_broadcast([32, 8, 64])
            else:
                cb_ = tab_c[32:64, 0:64].unsqueeze(1).to_broadcast([32, 8, 64])
                sb_ = tab_s[32:64, 0:64].unsqueeze(1).to_broadcast([32, 8, 64])
            v3 = lambda ap: ap.rearrange("p (r c) -> p r c", c=64)
            tt(v3(t1[p0:p0 + 32, :]), v3(qn[p0:p0 + 32, :]), cb_, ALU.mult, (('tmpf', 0), 'tab', 'tabS2'), (('tmpf', 1),))
            tt(v3(qn[p0:p0 + 32, :]), v3(ps[6][p0:p0 + 32, :]), sb_, ALU.mult, (('ps', 6), 'tab', 'tabS2', ('tmpf', 0)), (('tmpf', 0),))
        tt(out_bf, t1[0:64, :], qn[0:64, :], ALU.add, (('tmpf', 0), ('tmpf', 1)), out_keys)

    def head_norm_rope(*a, **kw):
        for _ in head_norm_rope_gen(*a, **kw):
            pass

    ssm_state = {'modgen': iter(())}

    def ssm_batch(l, ct, UTv, YSv):
        o = [0]

        def carve(n, dt=BF16, shape=None):
            n_bf = n * (2 if dt == F32 else 1)
            v = scr_view(o[0], n_bf, dt)
            o[0] += n_bf
            return v
        VT = carve(2 * 8 * 256, F32).rearrange("p (r g k) -> p r g k", r=2, g=8)
        M2 = wB[:, :].bitcast(F32).rearrange("p (r g m) -> p r g m", r=2, g=8)
        zoff = o[0]
        Zb = carve(2 * 8 * 128).rearrange("p (r g m) -> p r g m", r=2, g=8)
        N1 = carve(8 * 256).rearrange("p (g q) -> p g q", g=8)
        Kc = carve(8 * 256).rearrange("p (g d m) -> p g d m", g=8, d=2)
        Ug = carve(2 * 256).rearrange("p (d k) -> p d k", d=2)
        M2b = carve(2 * 128).rearrange("p (r m) -> p r m", r=2)
        YTs = scr_view(zoff, 2 * 1024).rearrange("p (d m) -> p d m", d=2)
        prep0 = o[0]
        SM = carve(20 * 8, F32).rearrange("p (a g) -> p a g", g=8)
        PW = carve(2 * 9 * 8, F32).rearrange("p (r k g) -> p r k g", r=2, k=9)
        IP = carve(2 * 9 * 8, F32).rearrange("p (r k g) -> p r k g", r=2, k=9)
        PWd = carve(2 * 8 * 8, F32).rearrange("p (r k g) -> p r k g", r=2, k=8)
        IPd = carve(2 * 8 * 8, F32).rearrange("p (r k g) -> p r k g", r=2, k=8)
        Bt = carve(2 * 128, F32).rearrange("p (r g c) -> p r g c", r=2, g=8)
        Ct = carve(2 * 128, F32).rearrange("p (r g c) -> p r g c", r=2, g=8)
        T1 = carve(4 * 128, F32).rearrange("p (a g c) -> p a g c", a=4, g=8)
        prep1 = o[0]
        SS = carve(16, F32).rearrange("p (r g) -> p r g", r=2)
        AB = carve(32, F32).rearrange("p (a r g) -> p a r g", a=2, r=2)
        SF = carve(128, F32).rearrange("p (s r g) -> p s r g", s=8, r=2)
        SQ = carve(2 * 6 * 8, F32).rearrange("p (r m g) -> p r m g", r=2, m=6)
        EC = carve(16, F32).rearrange("p (r g) -> p r g", r=2)
        EIN = carve(128, F32).rearrange("p (r g s) -> p r g s", r=2, g=8)
        U5 = carve(5 * 16, F32).rearrange("p (a r g) -> p a r g", a=5, r=2)
        end_all = o[0]
        o[0] = prep0
        PT = carve(2 * 8 * 32, F32).rearrange("p (r g j) -> p r g j", r=2, g=8)
        CT = carve(2 * 256, F32).rearrange("p (a g j) -> p a g j", a=2, g=8)
        SSw = carve(128, F32).rearrange("p (r g s) -> p r g s", r=2, g=8)
        T2 = [carve(192, F32).rearrange("p (r g s) -> p r g s", r=3, g=8)]
        W12 = carve(256, F32).rearrange("p (a r g s) -> p a r g s", a=2, r=2, g=8)
        assert o[0] <= prep1, (o[0], prep1)
        o[0] = end_all
        T2.append(carve(192, F32).rearrange("p (r g s) -> p r g s", r=3, g=8))
        K = lambda *a: ('ssm',) + a
        sm = lambda i: SM[:, i, :]
        simple_load(SM[:, 0:3, :], lam_d[l, ct], K('lam'))
        simple_load(Bt[:].rearrange("p r g c -> p r (g c)"), sB_d[l, ct], K('Bt'))
        simple_load(Ct[:].rearrange("p r g c -> p r (g c)"), sC_d[l, ct], K('Ct'))
        simple_load(SS[:].rearrange("p r g -> p (r g)"), h0_d[l, ct], K('SS'))
        LRE, LIM, LDT, DTT, RHO, PHI, MAG, CC_, SS_, AR, AI, U1, U2, U3, U4, DEN, NRE, CRE, CIM, INV = range(20)
        sk = K('sm')
        act(sm(DTT), sm(LDT), AF.Exp, (K('lam'),), (sk,))
        tt(sm(RHO), sm(LRE), sm(DTT), ALU.mult, (sk, K('lam')), (sk,))
        tt(sm(PHI), sm(LIM), sm(DTT), ALU.mult, (sk, K('lam')), (sk,))
        act(sm(MAG), sm(RHO), AF.Exp, (sk,), (sk,), scale=1.0 / 32)
        act(sm(CC_), sm(PHI), AF.Sin, (sk, 'halfpi'), (sk,), scale=1.0 / 32, bias=halfpi[:])
        act(sm(SS_), sm(PHI), AF.Sin, (sk,), (sk,), scale=1.0 / 32)
        tt(sm(AR), sm(MAG), sm(CC_), ALU.mult, (sk,), (sk,))
        tt(sm(AI), sm(MAG), sm(SS_), ALU.mult, (sk,), (sk,))
        for _ in range(5):
            tt(sm(U1), sm(AR), sm(AR), ALU.mult, (sk,), (sk,))
            tt(sm(U2), sm(AI), sm(AI), ALU.mult, (sk,), (sk,))
            stt(sm(U3), sm(AR), 2.0, sm(AI), ALU.mult, ALU.mult, (sk,), (sk,))
            tt(sm(AR), sm(U1), sm(U2), ALU.subtract, (sk,), (sk,))
            cp(sm(AI), sm(U3), (sk,), (sk,))
        tt(sm(U1), sm(LRE), sm(LRE), ALU.mult, (sk, K('lam')), (sk,))
        tt(sm(U2), sm(LIM), sm(LIM), ALU.mult, (sk, K('lam')), (sk,))
        tt(sm(DEN), sm(U1), sm(U2), ALU.add, (sk,), (sk,))
        recip(sm(DEN), sm(DEN), (sk,), (sk,))
        ts(sm(NRE), sm(AR), -1.0, ALU.add, (sk,), (sk,))
        tt(sm(U1), sm(NRE), sm(LRE), ALU.mult, (sk,), (sk,))
        tt(sm(U2), sm(AI), sm(LIM), ALU.mult, (sk,), (sk,))
        tt(sm(U3), sm(U1), sm(U2), ALU.add, (sk,), (sk,))
        tt(sm(CRE), sm(U3), sm(DEN), ALU.mult, (sk,), (sk,))
        tt(sm(U1), sm(AI), sm(LRE), ALU.mult, (sk,), (sk,))
        tt(sm(U2), sm(NRE), sm(LIM), ALU.mult, (sk,), (sk,))
        tt(sm(U3), sm(U1), sm(U2), ALU.subtract, (sk,), (sk,))
        tt(sm(CIM), sm(U3), sm(DEN), ALU.mult, (sk,), (sk,))
        pk_ = K('pw')
        memset(PW[:, 0, 0, :], 1.0, (pk_,))
        memset(PW[:, 1, 0, :], 0.0, (pk_,))
        memset(IP[:, 0, 0, :], 1.0, (pk_,))
        memset(IP[:, 1, 0, :], 0.0, (pk_,))
        cp(PW[:, 0, 1, :], sm(AR), (sk,), (pk_,))
        cp(PW[:, 1, 1, :], sm(AI), (sk,), (pk_,))
        tt(sm(U1), sm(AR), sm(AR), ALU.mult, (sk,), (sk,))
        tt(sm(U2), sm(AI), sm(AI), ALU.mult, (sk,), (sk,))
        tt(sm(INV), sm(U1), sm(U2), ALU.add, (sk,), (sk,))
        recip(sm(INV), sm(INV), (sk,), (sk,))
        tt(IP[:, 0, 1, :], sm(AR), sm(INV), ALU.mult, (sk,), (pk_,))
        stt(IP[:, 1, 1, :], sm(AI), -1.0, sm(INV), ALU.mult, ALU.mult, (sk,), (pk_,))

        def cmul(dst_r, dst_i, xr, xi, yr, yi, rkeys, wkeys):
            tt(sm(U1), xr, yr, ALU.mult, rkeys + (sk,), (sk,))
            tt(sm(U2), xi, yi, ALU.mult, rkeys + (sk,), (sk,))
            tt(sm(U3), xr, yi, ALU.mult, rkeys + (sk,), (sk,))
            tt(sm(U4), xi, yr, ALU.mult, rkeys + (sk,), (sk,))
            tt(dst_r, sm(U1), sm(U2), ALU.subtract, (sk,), wkeys)
            tt(dst_i, sm(U3), sm(U4), ALU.add, (sk,), wkeys)
        for k in range(1, 8):
            cmul(PW[:, 0, k + 1, :], PW[:, 1, k + 1, :], PW[:, 0, k, :], PW[:, 1, k, :], PW[:, 0, 1, :], PW[:, 1, 1, :], (pk_,), (pk_,))
            cmul(IP[:, 0, k + 1, :], IP[:, 1, k + 1, :], IP[:, 0, k, :], IP[:, 1, k, :], IP[:, 0, 1, :], IP[:, 1, 1, :], (pk_,), (pk_,))
        dk_ = K('pwd')
        cp(PWd[0:64], PW[0:64, :, 1:9, :], (pk_,), (dk_,))
        cp(IPd[0:64], IP[0:64, :, 1:9, :], (pk_,), (dk_,))
        for j in range(8):
            cp(PWd[64:128, :, j, :], PW[64:128, :, 8 - j, :], (pk_,), (dk_,))
            cp(IPd[64:128, :, j, :], IP[64:128, :, 8 - j, :], (pk_,), (dk_,))
        cp(AB[:, 0, 0, :], PW[:, 0, 8, :], (pk_,), (K('AB'),))
        cp(AB[:, 0, 1, :], PW[:, 0, 8, :], (pk_,), (K('AB'),))
        ts(AB[:, 1, 0, :], PW[:, 1, 8, :], -1.0, ALU.mult, (pk_,), (K('AB'),))
        cp(AB[:, 1, 1, :], PW[:, 1, 8, :], (pk_,), (K('AB'),))
        bc = lambda ap: ap.unsqueeze(2).to_broadcast([128, 8, 16])
        tk = K('T1')
        tt(T1[:, 0], Bt[:, 0], bc(sm(CRE)), ALU.mult, (K('Bt'), sk), (tk,))
        tt(T1[:, 1], Bt[:, 1], bc(sm(CIM)), ALU.mult, (K('Bt'), sk), (tk,))
        tt(T1[:, 2], Bt[:, 0], bc(sm(CIM)), ALU.mult, (K('Bt'), sk), (tk,))
        tt(T1[:, 3], Bt[:, 1], bc(sm(CRE)), ALU.mult, (K('Bt'), sk), (tk,))
        tt(Bt[:, 0], T1[:, 0], T1[:, 1], ALU.subtract, (tk,), (K('Bt'),))
        tt(Bt[:, 1], T1[:, 2], T1[:, 3], ALU.add, (tk,), (K('Bt'),))
        Zv = lambda r: Zb[:, r].rearrange("p g (i c) -> p g i c", i=8)
        M2v = lambda r: M2[:, r].rearrange("p g (j c) -> p g j c", j=8)
        def z_chain():
            for i in range(8):
                pr_, pi_ = bc(IPd[:, 0, i, :]), bc(IPd[:, 1, i, :])
                tt(T1[:, 0], Bt[:, 0], pr_, ALU.mult)
                yield
                tt(T1[:, 1], Bt[:, 1], pi_, ALU.mult)
                yield
                tt(Zv(0)[:, :, i, :], T1[:, 0], T1[:, 1], ALU.subtract)
                yield
                tt(T1[:, 0], Bt[:, 0], pi_, ALU.mult)
                yield
                tt(T1[:, 1], Bt[:, 1], pr_, ALU.mult)
                yield
                tt(Zv(1)[:, :, i, :], T1[:, 0], T1[:, 1], ALU.add)
                yield

        def m2_chain():
            for i in range(8):
                pr_, pi_ = bc(PWd[:, 0, i, :]), bc(PWd[:, 1, i, :])
                tt(T1[:, 2], Ct[:, 0], pr_, ALU.mult)
                yield
                tt(T1[:, 3], Ct[:, 1], pi_, ALU.mult)
                yield
                tt(M2v(0)[:, :, i, :], T1[:, 2], T1[:, 3], ALU.subtract)
                yield
                tt(T1[:, 2], Ct[:, 0], pi_, ALU.mult)
                yield
                tt(T1[:, 3], Ct[:, 1], pr_, ALU.mult)
                yield
                stt(M2v(1)[:, :, i, :], T1[:, 2], -1.0, T1[:, 3], ALU.mult, ALU.subtract)
                yield
        za, mb = z_chain(), m2_chain()
        for _ in range(48):
            next(za, None)
            next(mb, None)
        if SSM_SUB < 2:
            return
        for g in range(8):
            pn = [0 + (g % 2), 2 + (g % 2)]
            for d_ in range(2):
                for r in range(2):
                    mm(ps[pn[d_]][:, r * 64:(r + 1) * 64], Zb[d_ * 64:(d_ + 1) * 64, r, g, :],
                       identb[d_ * 64:(d_ + 1) * 64, d_ * 64:(d_ + 1) * 64], True, True)
                cp(N1[:, g, d_ * 128:(d_ + 1) * 128], ps[pn[d_]][:, 0:128], eng='act')
            cp(M2b[:, 0, :], M2[:, 0, g, :])
            cp(M2b[:, 1, :], M2[:, 1, g, :])
            pk2 = [4 + (g % 2), 6 + (g % 2)]
            for d_ in range(2):
                sl = slice(d_ * 64, (d_ + 1) * 64)
                for r in range(2):
                    mm(ps[pk2[d_]][:, 0:128], Zb[sl, r, g, :], M2b[sl, r, :], r == 0, r == 1)
            for d_ in range(2):
                stt(Kc[:, g, d_, :], ps[pk2[d_]][:, 0:128], -1.0, amask[:, d_, :], ALU.mult, ALU.mult)
            gg = ct * 8 + g
            stt(Kc[:, g, 0, :], ident[:], sDt[:, l, gg:gg + 1], Kc[:, g, 0, :], ALU.mult, ALU.add,
                ('ident', 'sDt', K('Kc', g)), (K('Kc', g),))

        def make_Ug(g):
            pu = next_ps(0, 3)
            for blk in range(2):
                mm(ps[pu][:, blk * 128:(blk + 1) * 128], UTv[:, blk, ct * 8 + g, :], identb[:], True, True,
                   (('UT', blk), 'identb'), (('ps', pu),))
                mm(ps[pu][:, 256 + blk * 128:256 + (blk + 1) * 128], UTv[:, 1 - blk, ct * 8 + g, :], jmatb[:], True, True,
                   (('UT', 1 - blk), 'jmatb'), (('ps', pu),))
            cp(Ug[:].rearrange("p d k -> p (d k)"), ps[pu][:], (('ps', pu),), (K('Ug'),), eng='act')
        if SSM_SUB < 3:
            return
        for g in range(8):
            make_Ug(g)
            pv_ = next_ps(3, 3)
            for d_ in range(2):
                for r in range(2):
                    q0 = (d_ * 2 + r) * 64
                    mm(ps[pv_][d_ * 64:(d_ + 1) * 64, r * 256:(r + 1) * 256], N1[:, g, q0:q0 + 64], Ug[:, d_, :], True, True,
                       (K('N1', g), K('Ug')), (('ps', pv_),))
            cp(VT[:, :, g, :], ps[pv_][:].rearrange("p (r k) -> p r k", r=2), (('ps', pv_),), (K('VT'),))
        if SSM_SUB < 4:
            return
        VTs = VT.rearrange("p r g (s j) -> p r g s j", s=8)
        tk_ = K('VT')
        bc4 = lambda ap: ap.unsqueeze(3).to_broadcast([128, 2, 8, 8])
        def cmul2(dst, x, y, tmp):
            tt(tmp[:, 0], x[:, 0], y[:, 0], ALU.mult)
            tt(tmp[:, 1], x[:, 1], y[:, 1], ALU.mult)
            tt(tmp[:, 2], x[:, 0], y[:, 1], ALU.mult)
            tt(tmp[:, 3], x[:, 1], y[:, 0], ALU.mult)
            tt(dst[:, 0], tmp[:, 0], tmp[:, 1], ALU.subtract)
            tt(dst[:, 1], tmp[:, 2], tmp[:, 3], ALU.add)

        def tab_gen():
            cp(SQ[:, 0, 0, :], AB[:, 0, 0, :])
            cp(SQ[:, 1, 0, :], AB[:, 1, 1, :])
            yield
            for m in range(5):
                cmul2(SQ[:, :, m + 1, :], SQ[:, :, m, :], SQ[:, :, m, :], U5[:, 0:4, 0, :])
                yield
            memset(PT[:, 0, :, 0:1], 1.0)
            memset(PT[:, 1, :, 0:1], 0.0)
            for m in range(5):
                n = 1 << m
                yb = lambda r: SQ[:, r, m, :].unsqueeze(2).to_broadcast([128, 8, n])
                tt(CT[:, 0, :, 0:n], PT[:, 0, :, 0:n], yb(0), ALU.mult)
                tt(CT[:, 1, :, 0:n], PT[:, 1, :, 0:n], yb(1), ALU.mult)
                tt(PT[:, 0, :, n:2 * n], CT[:, 0, :, 0:n], CT[:, 1, :, 0:n], ALU.subtract)
                yield
                tt(CT[:, 0, :, 0:n], PT[:, 0, :, 0:n], yb(1), ALU.mult)
                tt(CT[:, 1, :, 0:n], PT[:, 1, :, 0:n], yb(0), ALU.mult)
                tt(PT[:, 1, :, n:2 * n], CT[:, 0, :, 0:n], CT[:, 1, :, 0:n], ALU.add)
                yield
        tg = tab_gen()
        for _ in range(6):
            next(ssm_state['modgen'], None)
        memset(SSw[:], 0.0)
        for j in range(32):
            t2 = T2[j % 2]
            tt(t2[:, 0:2], SSw[:], VTs[:, :, :, :, j], ALU.add)
            next(tg, None)
            cp(t2[:, 2], t2[:, 0])
            tt(W12[:, 0], t2[:, 0:2], bc4(AB[:, 0]), ALU.mult)
            tt(W12[:, 1], t2[:, 1:3], bc4(AB[:, 1]), ALU.mult)
            tt(SSw[:], W12[:, 0], W12[:, 1], ALU.add)
            cp(VTs[:, :, :, :, j], t2[:, 0:2], eng='act')
        for _ in tg:
            pass
        ts(EC[:], SS[:], role[:, 0:1], ALU.mult)
        for sg_ in range(8):
            cp(EIN[:, :, :, sg_], EC[:])
            cmul2(U5[:, 4], SQ[:, :, 5, :], EC[:], U5[:, 0:4, 0, :])
            tt(SF[:, sg_], SSw[:, :, :, sg_], U5[:, 4], ALU.add)
            ts(EC[:], SF[:, sg_], role[:, 0:1], ALU.mult)
        store(sout_d[l, ct], SF[:].rearrange("p s r g -> p (s r g)"), (), K('SF'))
        for sg_ in range(8):
            eb = lambda r: EIN[:, r, :, sg_].unsqueeze(2).to_broadcast([128, 8, 32])
            tt(CT[:, 0], PT[:, 0], eb(0), ALU.mult)
            tt(CT[:, 1], PT[:, 1], eb(1), ALU.mult)
            tt(CT[:, 0], CT[:, 0], CT[:, 1], ALU.subtract)
            tt(VTs[:, 0, :, sg_, :], VTs[:, 0, :, sg_, :], CT[:, 0], ALU.add)
            tt(CT[:, 0], PT[:, 0], eb(1), ALU.mult)
            tt(CT[:, 1], PT[:, 1], eb(0), ALU.mult)
            tt(CT[:, 0], CT[:, 0], CT[:, 1], ALU.add)
            tt(VTs[:, 1, :, sg_, :], VTs[:, 1, :, sg_, :], CT[:, 0], ALU.add)
        if SSM_SUB < 5:
            return
        for tb in range(2):
            pyf = [4, 5]
            pyb = [6, 7]
            for g in range(8):
                make_Ug(g)
                bank, col = g // 4, (g % 4) * 128
                for (d_, pys, kb_) in ((0, pyf, tb), (1, pyb, 1 - tb)):
                    sl = slice(d_ * 64, (d_ + 1) * 64)
                    ksl = slice(kb_ * 128, (kb_ + 1) * 128)
                    o_ = ps[pys[bank]][:, col:col + 128]
                    mm(o_, VT[sl, 0, g, ksl], M2[sl, 0, g, :], True, False, (tk_, ('w', 'B', 0), ('w', 'B', 1)), (('ps', pys[bank]),))
                    mm(o_, VT[sl, 1, g, ksl], M2[sl, 1, g, :], False, False, (tk_, ('w', 'B', 0), ('w', 'B', 1)), (('ps', pys[bank]),))
                    mm(o_, Ug[:, d_, ksl], Kc[:, g, d_, :], False, True, (K('Ug'), K('Kc', g)), (('ps', pys[bank]),))
            for bank in range(2):
                for (d_, pys, eng_) in ((0, pyf, 'act'), (1, pyb, 'dve')):
                    cp(YTs[:, d_, :].rearrange("p (j g c) -> p g j c", j=8, g=8)[:, bank * 4:(bank + 1) * 4, :, :],
                       ps[pys[bank]][:].rearrange("p (g j c) -> p g j c", g=4, j=8), (('ps', pys[bank]),), (K('Z'),), eng=eng_)
            for jh in range(2):
                pt = next_ps(0, 3)
                for jj in range(4):
                    j = jh * 4 + jj
                    o_ = ps[pt][:, jj * 128:(jj + 1) * 128]
                    ltf = YTs[:, 0, j * 128:(j + 1) * 128]
                    ltb = YTs[:, 1, j * 128:(j + 1) * 128]
                    mm(o_, ltf, identb[:], True, False, (K('Z'), 'identb'), (('ps', pt),))
                    mm(o_, ltb, jmatb[:], False, True, (K('Z'), 'jmatb'), (('ps', pt),))
                y = ps[pt][:]
                y2 = tmpf[0]
                act(y2[:], y, AF.Square, (('ps', pt),), (('tmpf', 0),))
                ts(y2[:], y2[:], 0.044715, ALU.mult, (('tmpf', 0),), (('tmpf', 0),), s2=1.0, op1=ALU.add)
                tt(y2[:], y2[:], y, ALU.mult, (('tmpf', 0), ('ps', pt)), (('tmpf', 0),))
                sg = tmpf[1]
                act(sg[:], y2[:], AF.Sigmoid, (('tmpf', 0),), (('tmpf', 1),), scale=1.5957691216057308)
                dest = YSv[:, ct, tb * 1024:(tb + 1) * 1024].rearrange("p (k j) -> p j k", j=8)[:, jh * 4:(jh + 1) * 4, :]
                v3 = lambda ap: ap.rearrange("p (j k) -> p j k", j=4)
                tt(dest, v3(y), v3(sg[:]), ALU.mult, (('ps', pt), ('tmpf', 1)), tuple(('YH', ct, 2 * tb + cc) for cc in range(2)))

    def ssm_post(l, YH, YG):
        gv, gk = wload('B', wblock(w_glu_d[l], 0, 512))
        for nt_ in range(4):
            for c in range(NCH):
                cs = slice(c * CH, (c + 1) * CH)
                p_ = next_ps(0, 3)
                for kt in range(4):
                    mm(ps[p_][:], gv[:, kt, nt_ * 128:(nt_ + 1) * 128], YH[:, kt, cs], kt == 0, kt == 3,
                       (gk, ('YH', kt, c)), (('ps', p_),))
                sg = tmpf[2]
                act(sg[:], ps[p_][:], AF.Sigmoid, (('ps', p_), 'bglu'), (('tmpf', 2),), bias=bglu[:, l, nt_:nt_ + 1])
                tt(YG[:, nt_, cs], YH[:, nt_, cs], sg[:], ALU.mult, (('YH', nt_, c), ('tmpf', 2)), (('YG', nt_, c),))

    def attention(l, qh):
        YA = scr_view(0, 8 * 1024).rearrange("p (a t) -> p a t", t=1024)
        qaug = [scr_view(8192 + i * 1024, 1024) for i in range(2)]
        kaug = scr_view(8192 + 2048, NK * 128)
        vaug = scr_view(8192 + 2048 + NK * 128, NK * 256).rearrange("p (k a m) -> p k a m", k=NK, a=2)
        vst = scr_view(8192 + 2048 + NK * 128 + NK * 256, 2048, F32).rearrange("p (b m) -> p b m", m=64)
        chunks = (2 * qh, 2 * qh + 1)
        AK = lambda *a: ('att',) + a
        for i in range(2):
            dma('pool', qaug[i][64:74, :], qmask_d[:, qh * 1024:(qh + 1) * 1024], (), (AK('qm', i),), AK('qm', i))
        dma('pool', kaug[64:74, :], kmask_d, (), (AK('km'),), AK('km'))
        memset(vaug[:, :, 0, 64:128], 1.0, (AK('v1'),))
        memset(vaug[:, :, 1, 0:64], 1.0, (AK('v1'),))
        state = {'n': 0, 'pend': None, 'wq': None}

        def load_wq(hq):
            if hq % 4 == 0:
                state['wq'] = wload('B', wblock(w_in_d[l], 1536 + 64 * hq, 256))

        def prep_q_gen(hq, ci):
            wqv, wqk = state['wq']
            qa = qaug[hq % 2]
            c = chunks[ci]
            proj_hT(wqv, wqk, (hq % 4) * 64, 64, c, ps[5][0:64, :], ('ps', 5))
            yield from head_norm_rope_gen(ps[5][0:64, :], ('ps', 5), qkg[:, l, 0:1], c, qa[0:64, ci * CH:(ci + 1) * CH], ())

        def prep_q(hq):
            load_wq(hq)
            for ci in range(2):
                for _ in prep_q_gen(hq, ci):
                    pass

        def attn_loop(hq, ci, po, side=None, epi=None):
            par = hq % 2
            qa = qaug[hq % 2]
            LOOK = 2

            def emit_pv(kb):
                mm(ps[po][:], vaug[:, kb, par, :], tmpb[kb % 4][:], kb == 0, kb == NK - 1)
            for kb in range(NK):
                if side is not None and kb in (0, 9, 17):
                    next(side, None)
                if epi is not None and kb in (2, 10, 16):
                    next(epi, None)
                pss = kb % 3
                mm(ps[pss][:], kaug[0:74, kb * 128:(kb + 1) * 128], qa[0:74, ci * CH:(ci + 1) * CH], True, True)
                act(tmpb[kb % 4][:], ps[pss][:], AF.Exp, scale=0.125)
                if kb >= LOOK:
                    emit_pv(kb - LOOK)
            for kb in range(NK - LOOK, NK):
                emit_pv(kb)
            if side is not None:
                for _ in side:
                    pass
            if epi is not None:
                for _ in epi:
                    pass

        def epilogue_gen(hq, ci, po):
            par = hq % 2
            nsl = slice(par * 64, par * 64 + 64)
            dp = (1 - par) * 64
            rd = tmpf[2]
            recip(rd[dp:dp + 1, :], ps[po][dp:dp + 1, :])
            yield
            pbc = 5 if par == 0 else 6
            mm(ps[pbc][nsl, :], ones_f[dp:dp + 1, 0:64], rd[dp:dp + 1, :], True, True)
            bc_ = sq_t[1]
            cp(bc_[nsl, :], ps[pbc][nsl, :], eng='act')
            yield
            tt(YA[nsl, hq // 2, ci * CH:(ci + 1) * CH], ps[po][nsl, :], bc_[nsl, :], ALU.mult)

        def epilogue(hq, ci, po):
            for _ in epilogue_gen(hq, ci, po):
                pass

        for g in range(4):
            if state['pend'] is not None:
                epilogue(*state['pend'])
                state['pend'] = None
            wv, wk = wload('A', wblock(w_in_d[l], 2560 + 64 * g, 64))
            for c in range(NCH):
                cs = slice(c * CH, (c + 1) * CH)
                proj_hT(wv, wk, 0, 64, c, ps[5][0:64, :], ('ps', 5))
                kst = None
                if qh == 0:
                    def kst(qn, c=c, g=g):
                        store(kout_d[l, g, :, c * CH:(c + 1) * CH], qn[0:64, :], (('tmpf', 0),), ('tmpf', 0))
                head_norm_rope(ps[5][0:64, :], ('ps', 5), qkg[:, l, 1:2], c, kaug[0:64, cs], (AK('k', c),), kstore=kst)
            dma('pool', kaug[0:64, T:T + 256], ckT_d[l, g], (), (AK('kc'),), AK('kc'))
            wv, wk = wload('A', wblock(w_in_d[l], 2816 + 64 * g, 64))
            for tq in range(2):
                pv_ = next_ps(3, 2)
                for t8 in range(8):
                    tb = tq * 8 + t8
                    for kt in range(NT):
                        mm(ps[pv_][:, t8 * 64:(t8 + 1) * 64], hT[:, kt, tb * 128:(tb + 1) * 128], wv[:, kt, :],
                           kt == 0, kt == NT - 1)
                src8 = ps[pv_][:, 0:512].rearrange("p (b m) -> p b m", m=64)
                cp(vaug[:, tq * 8:(tq + 1) * 8, 0, 0:64], src8, eng='act')
                cp(vaug[:, tq * 8:(tq + 1) * 8, 1, 64:128], src8)
                if qh == 0:
                    cp(vst[:, tq * 8:(tq + 1) * 8, :], src8)
            if qh == 0:
                store(vout_d[l, g], vst[:].rearrange("p b m -> p (b m)"), (AK('vst'),), AK('vst'))
            for blk in range(2):
                dma('pool', vaug[:, 16 + blk, 0, 0:64], cv_d[l, :, blk, g * 64:(g + 1) * 64], (), (AK('vc'),), AK('vc'))
                dma('pool', vaug[:, 16 + blk, 1, 64:128], cv_d[l, :, blk, g * 64:(g + 1) * 64], (), (AK('vc'),), AK('vc'))
            for r_ in range(4):
                hq = 4 * g + r_
                if hq == 0:
                    prep_q(0)
                if hq + 1 < 16:
                    load_wq(hq + 1)
                for ci in range(2):
                    po = 3 + (state['n'] % 2)
                    state['n'] += 1
                    side = prep_q_gen(hq + 1, ci) if hq + 1 < 16 else None
                    epi = epilogue_gen(*state['pend']) if state['pend'] is not None else None
                    attn_loop(hq, ci, po, side, epi)
                    state['pend'] = (hq, ci, po)
        if state['pend'] is not None:
            epilogue(*state['pend'])
            state['pend'] = None
        merge_contrib(l, False, lambda kt, c: YA[:, kt, (c - 2 * qh) * CH:(c - 2 * qh + 1) * CH],
                      lambda kt, c: AK('ya', kt, c - 2 * qh), 8, wpa_d, 4608, chunks)

    for l in range(DEPTH):
        sh1 = lambda kt, l=l: modv[:, l, kt:kt + 1]
        a1 = lambda kt, l=l: A1[:, l, kt:kt + 1]
        if STAGE >= 1:
            rmsnorm_to_hT(a1, sh1, l)

        if ENABLE_SSM and STAGE >= 2:
            YH = mg_view(0, 4 * T).rearrange("p (a t) -> p a t", t=T)
            UT = mg_view(4 * T, 2 * 4096).rearrange("p (b f) -> p b f", f=4096)
            wv, wk = wload('A', wblock(w_in_d[l], 3072, 256))
            wv2, wk2 = wload('A', wblock(w_in_d[l], 3328, 256))
            for blk in range(2):
                for i in range(8):
                    p_ = next_ps(0, 3)
                    for hf, (wvv, wkk) in enumerate(((wv, wk), (wv2, wk2))):
                        for kt in range(NT):
                            lt = hT[:, kt, blk * 1024 + i:blk * 1024 + i + 1017:8]
                            mm(ps[p_][:, hf * 256:(hf + 1) * 256], lt, wvv[:, kt, :], kt == 0, kt == NT - 1,
                               (wkk,) + tuple(('hT', kt, c_) for c_ in (2 * blk, 2 * blk + 1)), (('ps', p_),))
                    cp(UT[:, blk, :].rearrange("p (g i c) -> p i g c", g=32, i=8)[:, i, :, :],
                       ps[p_][:].rearrange("p (g c) -> p g c", g=32), (('ps', p_),), (('UT', blk),), eng='act')
            UTv = UT.rearrange("p b (g m) -> p b g m", g=32)
            ssm_state['modgen'] = compute_mods_gen(l + 1) if l + 1 < DEPTH else iter(())
            for ct in range(4 if SSM_SUB >= 1 else 0):
                ssm_batch(l, ct, UTv, YH)
            for _ in ssm_state['modgen']:
                pass
            if SSM_SUB < 6:
                continue
            YG = scr_view(0, 4 * T).rearrange("p (a t) -> p a t", t=T)
            ssm_post(l, YH, YG)
            merge_contrib(l, True, lambda kt, c: YG[:, kt, c * CH:(c + 1) * CH], lambda kt, c: ('YG', kt, c), 4, wps_d, 5632,
                          range(NCH))

        if STAGE < 3:
            continue
        CC = scr_view(0, 4 * T).rearrange("p (a t) -> p a t", t=T)
        CB = scr_view(4 * T, 4 * T).rearrange("p (a t) -> p a t", t=T)
        for stage, col0 in ((0, 512), (1, 1024), (2, 0)):
            for hb in range(2):
                wv, wk = wload('A', wblock(w_in_d[l], col0 + hb * 256, 256))
                for t2_ in range(2):
                    t_ = hb * 2 + t2_
                    for c in range(NCH):
                        cs = slice(c * CH, (c + 1) * CH)
                        p_ = next_ps(0, 3)
                        proj_hT(wv, wk, t2_ * 128, 128, c, ps[p_][:], ('ps', p_))
                        if stage == 0:
                            cp(CC[:, t_, cs], ps[p_][:], (('ps', p_),), (('CC', t_, c),), eng='act')
                        elif stage == 1:
                            tt(CC[:, t_, cs], ps[p_][:], CC[:, t_, cs], ALU.mult, (('ps', p_), ('CC', t_, c)), (('CC', t_, c),))
                        else:
                            cp(CB[:, t_, cs], ps[p_][:], (('ps', p_),), (('CB', t_, c),), eng='act')
        for t_ in range(4):
            w0, w1, w2, bb = (convw[:, l, t_, i:i + 1] for i in range(4))
            tt(convx[:, t_, 0:1], w0, rolem1[:], ALU.mult, ('convw', 'rolem1'), (('convx', t_),))
            tt(convx[:, t_, 1:2], w2, rolem1[:], ALU.mult, ('convw', 'rolem1'), (('convx', t_),))
            for c in range(NCH):
                lo, hi = c * CH, (c + 1) * CH
                acc = tmpf[c % 2]
                ak = ('tmpf', c % 2)
                rk = [('CC', t_, cc_) for cc_ in range(max(0, c - 1), min(NCH, c + 2))]
                act(acc[:], CC[:, t_, lo:hi], AF.Identity, rk + ['convw'], (ak,), scale=w1, bias=bb)
                l0 = max(lo - 1, 0)
                o0 = l0 + 1 - lo
                stt(acc[:, o0:CH], CC[:, t_, l0:hi - 1], w0, acc[:, o0:CH], ALU.mult, ALU.add, rk + ['convw', ak], (ak,))
                h1 = min(hi + 1, T)
                n1 = h1 - (lo + 1)
                stt(acc[:, 0:n1], CC[:, t_, lo + 1:h1], w2, acc[:, 0:n1], ALU.mult, ALU.add, rk + ['convw', ak], (ak,))
                lcols = [b for b in (lo, lo + 256) if b > 0]
                for b in lcols:
                    stt(acc[:, b - lo:b - lo + 1], CC[:, t_, b - 1:b], convx[:, t_, 0:1], acc[:, b - lo:b - lo + 1],
                        ALU.mult, ALU.add, rk + [('convx', t_), ak], (ak,))
                rcols = [b for b in (lo + 255, lo + 511) if b < T - 1]
                for b in rcols:
                    stt(acc[:, b - lo:b - lo + 1], CC[:, t_, b + 1:b + 2], convx[:, t_, 1:2], acc[:, b - lo:b - lo + 1],
                        ALU.mult, ALU.add, rk + [('convx', t_), ak], (ak,))
                tt(CB[:, t_, lo:hi], acc[:], CB[:, t_, lo:hi], ALU.mult, (ak, ('CB', t_, c)), (('CB', t_, c),))
        merge_contrib(l, not ENABLE_SSM, lambda kt, c: CB[:, kt, c * CH:(c + 1) * CH], lambda kt, c: ('CB', kt, c), 4, wpc_d, 3584,
                      range(NCH))

        if ENABLE_ATTN and STAGE >= 4:
            for qh in range(2):
                attention(l, qh)

        if STAGE < 5:
            continue
        for nb in range(4):
            wv, wk = wload('A', wblock(w_out_d[l], nb * 256, 256))
            for t_ in range(2):
                dt_ = nb * 2 + t_
                for c in range(NCH):
                    cs = slice(c * CH, (c + 1) * CH)
                    p_ = next_ps(0, 3)
                    for kt in range(NT):
                        mm(ps[p_][:], wv[:, kt, t_ * 128:(t_ + 1) * 128], mergedv[:, kt, cs], kt == 0, kt == NT - 1,
                           (wk, ('mg', kt, c)), (('ps', p_),))
                    stt(xT[:, dt_, cs], ps[p_][:], modv[:, l, 16 + dt_:17 + dt_], xT[:, dt_, cs], ALU.mult, ALU.add,
                        (('ps', p_), ('modv', l), ('x', dt_)), (('x', dt_),))

        if STAGE < 6:
            continue
        sh2 = lambda kt, l=l: modv[:, l, 24 + kt:25 + kt]
        a2 = lambda kt, l=l: A2[:, l, kt:kt + 1]
        rmsnorm_to_hT(a2, sh2, l)
        ACT_m = mg_view(0, 16 * 1024).rearrange("p (f t) -> p f t", t=1024)
        ACT_s = scr_view(0, 6 * 1024).rearrange("p (f t) -> p f t", t=1024)

        def act_tile(ft, lo, hi):
            return ACT_m[:, ft, lo:hi] if ft < 16 else ACT_s[:, ft - 16, lo:hi]
        modgen = iter(())
        for half in range(2):
            for fb in range(NFT // 2):
                next(modgen, None)
                gv, gk = wload('A', wblock(wfg_d[l], fb * 256, 256))
                uv, uk = wload('B', wblock(wfu_d[l], fb * 256, 256))
                for t_ in range(2):
                    ft = fb * 2 + t_
                    for c2 in range(2):
                        c = half * 2 + c2
                        pg = next_ps(0, 3)
                        proj_hT(gv, gk, t_ * 128, 128, c, ps[pg][:], ('ps', pg))
                        pu = next_ps(3, 3)
                        proj_hT(uv, uk, t_ * 128, 128, c, ps[pu][:], ('ps', pu))
                        sg = tmpb[3]
                        act(sg[:], ps[pg][:], AF.Silu, (('ps', pg),), (('tmpb', 3),))
                        tt(act_tile(ft, c2 * CH, (c2 + 1) * CH), ps[pu][:], sg[:], ALU.mult, (('ps', pu), ('tmpb', 3)),
                           (('ACT', ft, c2),))
            for dt_ in range(NT):
                pool_ = 'A' if dt_ % 2 == 0 else 'B'
                dv, dk = wload(pool_, wblock(wfd_d[l], dt_ * 128, 128), full=True)
                dk2 = ('w', pool_, 1)
                for c2 in range(2):
                    c = half * 2 + c2
                    cs = slice(c * CH, (c + 1) * CH)
                    p_ = next_ps(0, 3)
                    for ft in range(NFT):
                        mm(ps[p_][:], dv[:, ft, :], act_tile(ft, c2 * CH, (c2 + 1) * CH), ft == 0, ft == NFT - 1,
                           (dk, dk2, ('ACT', ft, c2)), (('ps', p_),))
                    stt(xT[:, dt_, cs], ps[p_][:], modv[:, l, 40 + dt_:41 + dt_], xT[:, dt_, cs], ALU.mult, ALU.add,
                        (('ps', p_), ('modv', l), ('x', dt_)), (('x', dt_),))
        for _ in modgen:
            pass

    for c in range(NCH):
        cs = slice(c * CH, (c + 1) * CH)
        rms_stats(c)
        for kt in range(NT):
            tf = tmpf[kt % 3]
            stt(tf[:], xT[:, kt, cs], fng[:, kt:kt + 1], rstd[:], ALU.mult, ALU.mult, (('x', kt), 'rstd', 'fng'),
                (('tmpf', kt % 3),))
            store(yT_d[kt * 128:(kt + 1) * 128, cs], tf[:], (('tmpf', kt % 3),), ('tmpf', kt % 3))

    S.emit(es, store_keys)
    _STATS['nops'] = len(S.ops)
    es.close()
    return nc, es


_PROG = {}


def _consts():
    ident = np.eye(128, dtype=np.float32)
    rot = np.zeros((64, 64), np.float32)
    for b0 in (0, 32):
        for j in range(16):
            rot[b0 + 16 + j, b0 + j] = -1.0
            rot[b0 + j, b0 + 16 + j] = 1.0
    ii = np.arange(128) // 16
    amask = np.zeros((128, 2, 128), np.float32)
    amask[:, 0, :] = (ii[:, None] > ii[None, :])
    amask[:, 1, :] = (ii[:, None] < ii[None, :])
    jmat = np.ascontiguousarray(np.eye(128, dtype=np.float32)[::-1])
    return ident, rot, amask, jmat


def _core_inputs(core, inp, shared):
    f32 = np.float32
    prompt = core < 4
    d = dict(shared)
    if prompt:
        x = inp['x_prompt'][core * 8:(core + 1) * 8].reshape(T, D)
        cv = inp['c_ctx']
    else:
        x = inp['x_sample'][core - 4]
        cv = inp['c'][core - 4]
    d['xT'] = np.ascontiguousarray(x.T)
    d['cvec'] = np.ascontiguousarray(cv.reshape(NT, 128).T)
    j = np.arange(64) % 16
    inv = (10000.0 ** (-(j.astype(np.float64)) / 16.0)).astype(f32)
    d['freq'] = (np.zeros((64, 1), f32) if prompt else inv.reshape(64, 1).copy())
    d['role'] = np.full((128, 1), 0.0 if prompt else 1.0, f32)
    seg_q = np.arange(T) // 256
    qm = np.zeros((10, T), f32)
    km = np.zeros((10, NK * 128), f32)
    seg_k = np.concatenate([np.arange(T) // 256, np.full(256, 8)])
    for s in range(9):
        km[s] = (seg_k == s)
        if prompt:
            qm[s] = BIG * ((seg_q == s) if s < 8 else 0.0)
        else:
            qm[s] = BIG
    km[9] = 1.0
    qm[9] = -BIG
    d['qmask'] = qm
    d['kmask'] = km
    if prompt:
        d['ckT'] = np.zeros((DEPTH, 4, 64, 256), f32)
        d['cv'] = np.zeros((DEPTH, 128, 2, 256), f32)
        d['h0'] = np.zeros((DEPTH, 4, 128, 16), f32)
    else:
        b = core - 4
        ck = inp['cache_k'][b]
        d['ckT'] = np.ascontiguousarray(ck.transpose(0, 2, 3, 1))
        cvv = inp['cache_v'][b]
        d['cv'] = np.ascontiguousarray(cvv.reshape(DEPTH, 2, 128, 256).transpose(0, 2, 1, 3))
        d['h0'] = _state_layout(inp['state_ssm_re'][b], inp['state_ssm_im'][b])
    return d


def _tile_layout(a):
    L = a.shape[0]
    rest = a.shape[4:]
    a = a.reshape((L, 2, 4, 8, 64) + rest)
    nd = a.ndim
    perm = (0, 2, 1, 4, 3) + tuple(range(5, nd))
    a = a.transpose(perm)
    return np.ascontiguousarray(a.reshape((L, 4, 128, 8) + rest))


def _state_layout(sre, sim):
    r = _tile_layout(sre)
    i = _tile_layout(sim)
    return np.ascontiguousarray(np.stack([r, i], axis=3).reshape(DEPTH, 4, 128, 16))


def _shared_inputs(inp):
    f32 = np.float32
    ident, rot, amask, jmat = _consts()
    d = {'ident': ident, 'rot': rot, 'amask': amask, 'jmat': jmat}
    d['w_ada'] = inp['w_ada']
    d['b_ada'] = np.ascontiguousarray(inp['b_ada'].reshape(DEPTH, 48, 128).transpose(0, 2, 1))
    d['n1g'] = np.ascontiguousarray(inp['norm1_g'].reshape(DEPTH, NT, 128).transpose(0, 2, 1))
    d['n2g'] = np.ascontiguousarray(inp['norm2_g'].reshape(DEPTH, NT, 128).transpose(0, 2, 1))
    d['fng'] = np.ascontiguousarray(inp['final_norm_g'].reshape(NT, 128).T)
    d['w_in'] = inp['w_in']
    cw = np.concatenate([inp['conv_w'], inp['conv_b'][:, None, :]], axis=1)
    d['convw'] = np.ascontiguousarray(cw.reshape(DEPTH, 4, 4, 128).transpose(0, 3, 2, 1))
    d['qkg'] = np.ascontiguousarray(np.stack([inp['q_norm_g'], inp['k_norm_g']], axis=2))
    lre = _tile_layout(inp['ssm_lambda_re'])
    lim = _tile_layout(inp['ssm_lambda_im'])
    ldt = _tile_layout(np.broadcast_to(inp['ssm_log_dt'][..., None], (DEPTH, 2, 32, 64)).copy())
    d['lam'] = np.ascontiguousarray(np.stack([lre, lim, ldt], axis=3))
    bre = _tile_layout(inp['ssm_b_re']).reshape(DEPTH, 4, 128, 128)
    bim = _tile_layout(inp['ssm_b_im']).reshape(DEPTH, 4, 128, 128)
    d['sB'] = np.ascontiguousarray(np.stack([bre, bim], axis=3))
    cre = _tile_layout(np.ascontiguousarray(inp['ssm_c_re'].transpose(0, 1, 2, 4, 3))).reshape(DEPTH, 4, 128, 128)
    cim = _tile_layout(np.ascontiguousarray(inp['ssm_c_im'].transpose(0, 1, 2, 4, 3))).reshape(DEPTH, 4, 128, 128)
    d['sC'] = np.ascontiguousarray(np.stack([cre, cim], axis=3))
    sd = inp['ssm_d'].reshape(DEPTH, 32, 16)
    d['sD'] = np.ascontiguousarray(np.broadcast_to(sd.transpose(0, 2, 1)[:, None], (DEPTH, 8, 16, 32)).reshape(DEPTH, 128, 32))
    d['w_glu'] = inp['w_glu']
    d['b_glu'] = np.ascontiguousarray(inp['b_glu'].reshape(DEPTH, 4, 128).transpose(0, 2, 1))
    d['wpc'] = inp['w_proj_conv']
    d['wpa'] = inp['w_proj_attn']
    d['wps'] = inp['w_proj_ssm']
    d['w_out'] = inp['w_out']
    d['wfg'] = inp['w_ffn_gate']
    d['wfu'] = inp['w_ffn_up']
    d['wfd'] = inp['w_ffn_down']
    return {k: np.ascontiguousarray(v, dtype=f32) for k, v in d.items()}


def kernel(**inputs):
    inp = {k: np.asarray(v) for k, v in inputs.items()}
    if 'nc' not in _PROG:
        _PROG['nc'], _PROG['es'] = build_program()
    nc = _PROG['nc']
    shared = _shared_inputs(inp)
    in_maps = [_core_inputs(c, inp, shared) for c in range(8)]
    res = run_bass_kernel_spmd(nc, in_maps, core_ids=list(range(8)))
    R = res.results
    f32 = np.float32
    y_prompt = np.concatenate([R[c]['yT'].T.reshape(8, 256, D) for c in range(4)], axis=0).astype(f32)
    y_sample = np.stack([R[c]['yT'].T for c in range(4, 8)], axis=0).astype(f32)
    nk = np.concatenate([R[c]['kout'].reshape(DEPTH, 4, 64, 8, 256).transpose(3, 0, 4, 1, 2) for c in range(4)], axis=0)
    nv = np.concatenate([R[c]['vout'].reshape(DEPTH, 4, 128, 16, 64).transpose(0, 3, 2, 1, 4).reshape(DEPTH, 8, 256, 4, 64)
                         .transpose(1, 0, 2, 3, 4) for c in range(4)], axis=0)
    sre, sim = [], []
    for c in range(4):
        s = R[c]['sout'].reshape(DEPTH, 4, 2, 64, 8, 2, 8)
        s = s.transpose(4, 0, 5, 2, 1, 6, 3)
        s = s.reshape(8, DEPTH, 2, 2, 32, 64).copy()
        s[:, :, :, 1] = s[::-1, :, :, 1]
        sre.append(s[:, :, 0])
        sim.append(s[:, :, 1])
    new_re = np.concatenate(sre, axis=0)
    new_im = np.concatenate(sim, axis=0)
    return (y_prompt, y_sample, np.ascontiguousarray(nk, dtype=f32), np.ascontiguousarray(nv, dtype=f32),
            np.ascontiguousarray(new_re, dtype=f32), np.ascontiguousarray(new_im, dtype=f32))
```

```python
import math
import numpy as np
import concourse.bass as bass
import concourse.mybir as mybir
from concourse.bass_utils import run_bass_kernel_spmd
from contextlib import ExitStack

F32 = mybir.dt.float32
BF16 = mybir.dt.bfloat16
AF = mybir.ActivationFunctionType
ALU = mybir.AluOpType

D = 1024
T = 2048
DEPTH = 4
NT = 8
CH = 512
NCH = T // CH
INC = 6656
DFF = 2816
NFT = DFF // 128
EPS = 1e-6
BIG = 1024.0
NK = 18
NCK = 256
SAME_SYNC = True
ENABLE_SSM = True
ENABLE_ATTN = True
EPOCH = 12000
_STATS = {}
STAGE = 6
SSM_SUB = 6


_DT_SIZE = {str(F32): 4, str(BF16): 2}


def _region(ap):
    name = ap.tensor.name
    esz = _DT_SIZE[str(ap.dtype)]
    pairs = [(int(s_), int(c_)) for s_, c_ in ap.ap]
    off = int(ap.offset)
    if type(ap.tensor).__name__.startswith('DRam'):
        span = sum((c_ - 1) * abs(s_) for s_, c_ in pairs)
        return (name, 0, 1, off * esz, (off + span + 1) * esz)
    pstride, pcount = pairs[0]
    if type(ap.tensor).__name__.startswith('PSum'):
        return (name, 0, 128, 0, 2048)
    if pstride == 0:
        pstride = int(ap.tensor.shape[-1]) if len(ap.tensor.shape) == 2 else 1
        for d_ in ap.tensor.shape[1:]:
            pass
    p0 = off // pstride if pstride else 0
    f0 = off - p0 * pstride
    span = sum((c_ - 1) * abs(s_) for s_, c_ in pairs[1:])
    return (name, p0, p0 + pcount, f0 * esz, (f0 + span + 1) * esz)


def _overlap(a, b):
    return a[1] < b[2] and b[1] < a[2] and a[3] < b[4] and b[3] < a[4]


def _covers(a, b):
    return a[1] <= b[1] and a[2] >= b[2] and a[3] <= b[3] and a[4] >= b[4]


class Sched:
    def __init__(self, nc):
        self.nc = nc
        self.ops = []
        self.recs = {}
        self.dma_count = {}

    def add(self, eng, fn, reads=(), writes=(), dma=False, dkey=None, rd_aps=(), wr_aps=()):
        idx = len(self.ops)
        raw, oth, dwaits = set(), set(), {}

        def dep(p, kind):
            if p is None or p == idx:
                return
            po = self.ops[p]
            if po['dma']:
                dwaits[po['dkey']] = self.dma_count[po['dkey']] * 16
            elif kind == 'raw':
                raw.add(p)
            else:
                oth.add(p)
        rregs = [_region(a) for a in rd_aps]
        wregs = [_region(a) for a in wr_aps]
        for R in rregs:
            for rec in self.recs.get(R[0], ()):
                if rec[1] is not None and _overlap(rec[0], R):
                    dep(rec[1], 'raw')
                if R[0].startswith('ps') and _overlap(rec[0], R):
                    for e2, rd in rec[2].items():
                        if e2 != eng:
                            dep(rd, 'war')
        for R in wregs:
            for rec in self.recs.get(R[0], ()):
                if _overlap(rec[0], R):
                    dep(rec[1], 'waw')
                    for rd in rec[2].values():
                        dep(rd, 'war')
        op = dict(eng=eng, fn=fn, raw=raw, oth=oth, dwaits=dwaits, dma=dma, dkey=dkey, signal=False)
        if dma:
            self.dma_count[dkey] = self.dma_count.get(dkey, 0) + 1
        self.ops.append(op)
        for R in rregs:
            lst = self.recs.setdefault(R[0], [])
            for rec in lst:
                if rec[0] == R:
                    rec[2][eng] = idx
                    break
            else:
                lst.append([R, None, {eng: idx}])
        for R in wregs:
            lst = self.recs.setdefault(R[0], [])
            lst[:] = [rec for rec in lst if not _covers(R, rec[0])]
            lst.append([R, idx, {}])
        return idx

    def emit(self, es, final_store_keys):
        nc = self.nc
        for op in self.ops:
            for p in op['raw']:
                po = self.ops[p]
                if po['eng'] != op['eng'] or (SAME_SYNC and op['eng'] in ('act', 'dve', 'pool')):
                    po['signal'] = True
            for p in op['oth']:
                po = self.ops[p]
                if po['eng'] != op['eng'] or (SAME_SYNC and op['eng'] in ('act', 'dve', 'pool')):
                    po['signal'] = True
        cnt = {}
        for op in self.ops:
            if op['signal']:
                c = cnt.get(op['eng'], 0)
                op['ticket'] = (c // EPOCH, c % EPOCH + 1)
                cnt[op['eng']] = c + 1
        sems = {}

        def sem(key):
            if key not in sems:
                sems[key] = es.enter_context(nc.semaphore("s%d" % len(sems)))
            return sems[key]
        for e, c in cnt.items():
            for ep in range(c // EPOCH + 1):
                sem(('e', e, ep))
        for k in self.dma_count:
            sem(('d', k))
        block = es.enter_context(nc.Block())
        engmap = [('pe', block.tensor), ('act', block.scalar), ('dve', block.vector),
                  ('pool', block.gpsimd), ('sp', block.sync)]
        ops = self.ops
        dma_count = self.dma_count
        for ename, deco in engmap:
            ops_e = [op for op in ops if op['eng'] == ename]

            def body(eng, ops_e=ops_e, ename=ename):
                waited = {}
                for op in ops_e:
                    waits = {}
                    for p in list(op['raw']) + list(op['oth']):
                        po = ops[p]
                        if not po['signal']:
                            continue
                        if po['eng'] == ename and not (SAME_SYNC and ename in ('act', 'dve', 'pool')):
                            continue
                        k = ('e', po['eng'], po['ticket'][0])
                        waits[k] = max(waits.get(k, 0), po['ticket'][1])
                    for dk, v in op['dwaits'].items():
                        k = ('d', dk)
                        waits[k] = max(waits.get(k, 0), v)
                    for k, v in waits.items():
                        if waited.get(k, 0) >= v:
                            continue
                        if k[0] == 'e':
                            newer = [kk for kk in waited if kk[0] == 'e' and kk[1] == k[1] and kk[2] > k[2]]
                            if newer:
                                continue
                        eng.wait_ge(sems[k], v)
                        waited[k] = v
                    ins = op['fn'](eng)
                    if op['dma']:
                        ins.then_inc(sems[('d', op['dkey'])], 16)
                    elif op['signal']:
                        ins.then_inc(sems[('e', ename, op['ticket'][0])], 1)
                if ename == 'sp':
                    for dk in final_store_keys:
                        eng.wait_ge(sems[('d', dk)], dma_count[dk] * 16)
            deco(body)


def build_program():
    nc = bass.Bass("TRN2", target_bir_lowering=False)
    S = Sched(nc)
    es = ExitStack()

    def din(name, shape):
        return nc.dram_tensor(name, list(shape), F32, kind="ExternalInput").ap()

    def dout(name, shape):
        return nc.dram_tensor(name, list(shape), F32, kind="ExternalOutput").ap()

    xT_d = din("xT", [D, T])
    cvec_d = din("cvec", [128, NT])
    freq_d = din("freq", [64, 1])
    role_d = din("role", [128, 1])
    qmask_d = din("qmask", [10, T])
    kmask_d = din("kmask", [10, NK * 128])
    ckT_d = din("ckT", [DEPTH, 4, 64, 256])
    cv_d = din("cv", [DEPTH, 128, 2, 256])
    h0_d = din("h0", [DEPTH, 4, 128, 16])
    ident_d = din("ident", [128, 128])
    jmat_d = din("jmat", [128, 128])
    rot_d = din("rot", [64, 64])
    amask_d = din("amask", [128, 2, 128])
    w_ada_d = din("w_ada", [DEPTH, D, 6 * D])
    b_ada_d = din("b_ada", [DEPTH, 128, 48])
    n1g_d = din("n1g", [DEPTH, 128, NT])
    n2g_d = din("n2g", [DEPTH, 128, NT])
    fng_d = din("fng", [128, NT])
    w_in_d = din("w_in", [DEPTH, D, INC])
    convw_d = din("convw", [DEPTH, 128, 4, 4])
    qkg_d = din("qkg", [DEPTH, 64, 2])
    lam_d = din("lam", [DEPTH, 4, 128, 3, 8])
    sB_d = din("sB", [DEPTH, 4, 128, 2, 128])
    sC_d = din("sC", [DEPTH, 4, 128, 2, 128])
    sD_d = din("sD", [DEPTH, 128, 32])
    w_glu_d = din("w_glu", [DEPTH, 512, 512])
    b_glu_d = din("b_glu", [DEPTH, 128, 4])
    wpc_d = din("wpc", [DEPTH, 512, D])
    wpa_d = din("wpa", [DEPTH, D, D])
    wps_d = din("wps", [DEPTH, 512, D])
    w_out_d = din("w_out", [DEPTH, D, D])
    wfg_d = din("wfg", [DEPTH, D, DFF])
    wfu_d = din("wfu", [DEPTH, D, DFF])
    wfd_d = din("wfd", [DEPTH, DFF, D])

    yT_d = dout("yT", [D, T])
    kout_d = dout("kout", [DEPTH, 4, 64, T])
    vout_d = dout("vout", [DEPTH, 4, 128, 16 * 64])
    sout_d = dout("sout", [DEPTH, 4, 128, 128])

    def sb(name, shape, dt=F32):
        return es.enter_context(nc.sbuf_tensor("sb_" + name, list(shape), dt))

    xT = sb("xT_sb", [128, NT, T])
    hT = sb("hT_sb", [128, NT, T], BF16)
    merged = sb("merged", [128, NT * T], BF16)
    wA = sb("wA", [128, 4096], BF16)
    wB = sb("wB", [128, 4096], BF16)
    SCR_N = 20 * 1024
    scr = sb("scr", [128, SCR_N], BF16)
    ident = sb("ident", [128, 128])
    identb = sb("identb", [128, 128], BF16)
    jmatb = sb("jmatb", [128, 128], BF16)
    ones_f = sb("ones_f", [128, 128])
    onesb = sb("onesb", [64, 64], BF16)
    sqb = sb("sqb", [64, 512], BF16)
    rotm = sb("rotm", [64, 64])
    amask = sb("amask", [128, 2, 128])
    modv = sb("modv", [128, DEPTH, 48])
    badat = sb("badat", [128, DEPTH, 48])
    A1 = sb("A1", [128, DEPTH, NT])
    A2 = sb("A2", [128, DEPTH, NT])
    n1g = sb("n1g", [128, DEPTH, NT])
    n2g = sb("n2g", [128, DEPTH, NT])
    fng = sb("fng", [128, NT])
    cvec = sb("cvec", [128, NT])
    scb = sb("scb", [128, NT], BF16)
    epst = sb("epst", [128, 1])
    halfpi = sb("halfpi", [128, 1])
    role = sb("role", [128, 1])
    rolem1 = sb("rolem1", [128, 1])
    freq = sb("freq", [64, 1])
    pw = sb("pw", [64, 2, 8])
    tabC = sb("tabC", [64, 64])
    tabS = sb("tabS", [64, 64])
    ttmp = sb("ttmp", [64, 64])
    convw = sb("convw", [128, DEPTH, 4, 4])
    convx = sb("convx", [128, 4, 2])
    qkg = sb("qkg", [64, DEPTH, 2])
    bglu = sb("bglu", [128, DEPTH, 4])
    sDt = sb("sDt", [128, DEPTH, 32])
    sq_t = [sb("sq%d" % i, [128, 512]) for i in range(2)]
    rstd = sb("rstd", [128, 512])
    tmpf = [sb("tmpf%d" % i, [128, 512]) for i in range(3)]
    tmpb = [sb("tmpb%d" % i, [128, 512], BF16) for i in range(4)]
    ps = [es.enter_context(nc.psum_tensor("ps%d" % i, [128, 512], F32)) for i in range(8)]

    def _aps(*xs):
        return [x for x in xs if isinstance(x, bass.AP)]

    def mm(out, lhsT, rhs, start, stop, reads=(), writes=(), **kw):
        S.add('pe', lambda e: e.matmul(out, lhsT=lhsT, rhs=rhs, start=start, stop=stop, **kw),
              rd_aps=_aps(lhsT, rhs), wr_aps=_aps(out))

    def act(out, in_, func, reads=(), writes=(), **kw):
        S.add('act', lambda e: e.activation(out=out, in_=in_, func=func, **kw),
              rd_aps=_aps(in_, kw.get('bias'), kw.get('scale')), wr_aps=_aps(out))

    def tt(out, in0, in1, op, reads=(), writes=(), eng='dve'):
        S.add(eng, lambda e: e.tensor_tensor(out=out, in0=in0, in1=in1, op=op), rd_aps=_aps(in0, in1), wr_aps=_aps(out))

    def ts(out, in0, s1, op0, reads=(), writes=(), s2=None, op1=None, eng='dve'):
        if op1 is None:
            S.add(eng, lambda e: e.tensor_scalar(out=out, in0=in0, scalar1=s1, scalar2=None, op0=op0),
                  rd_aps=_aps(in0, s1), wr_aps=_aps(out))
        else:
            S.add(eng, lambda e: e.tensor_scalar(out=out, in0=in0, scalar1=s1, scalar2=s2, op0=op0, op1=op1),
                  rd_aps=_aps(in0, s1, s2), wr_aps=_aps(out))

    def stt(out, in0, scalar, in1, op0, op1, reads=(), writes=()):
        S.add('dve', lambda e: e.scalar_tensor_tensor(out=out, in0=in0, scalar=scalar, in1=in1, op0=op0, op1=op1),
              rd_aps=_aps(in0, scalar, in1), wr_aps=_aps(out))

    def cp(out, in_, reads=(), writes=(), eng='dve'):
        if eng == 'act':
            S.add('act', lambda e: e.activation(out=out, in_=in_, func=AF.Copy), rd_aps=_aps(in_), wr_aps=_aps(out))
        else:
            S.add(eng, lambda e: e.tensor_copy(out=out, in_=in_), rd_aps=_aps(in_), wr_aps=_aps(out))

    def recip(out, in_, reads=(), writes=()):
        S.add('dve', lambda e: e.reciprocal(out=out, in_=in_), rd_aps=_aps(in_), wr_aps=_aps(out))

    def memset(ap, val, writes=(), eng='dve'):
        S.add(eng, lambda e: e.memset(ap, val), wr_aps=_aps(ap))

    def dma(q, out, in_, reads, writes, dkey):
        S.add(q, lambda e: e.dma_start(out=out, in_=in_), dma=True, dkey=dkey, rd_aps=_aps(in_), wr_aps=_aps(out))

    store_keys = []

    def store(out, in_, reads, key):
        dk = ('st', key)
        if dk not in store_keys:
            store_keys.append(dk)
        dma('sp', out, in_, reads, (), dk)

    wstate = {'A': 0, 'B': 0}

    def wload(pool, dram_ap, full=False):
        tile_ = wA if pool == 'A' else wB
        kt_, ncol = dram_ap.shape[1], dram_ap.shape[2]
        n = kt_ * ncol
        if full:
            assert n <= 4096
            keys = (('w', pool, 0), ('w', pool, 1))
            view = tile_[:, 0:n]
        else:
            assert n <= 2048
            i = wstate[pool] % 2
            wstate[pool] += 1
            keys = (('w', pool, i),)
            view = tile_[:, i * 2048:i * 2048 + n]
        view = view.rearrange("p (a b) -> p a b", b=ncol)
        dma('pool', view, dram_ap, (), keys, keys[0])
        return view, keys[0]

    def wblock(w_ap, c0, ncols):
        return w_ap[:, c0:c0 + ncols].rearrange("(kt p) c -> p kt c", p=128)

    def simple_load(tile_ap, dram_ap, key, q='sp'):
        dma(q, tile_ap, dram_ap, (), (key,), key)

    psrot = {'i': 0}

    def next_ps(lo, n):
        i = lo + (psrot['i'] % n)
        psrot['i'] += 1
        return i

    def scr_view(off, n, dt=BF16):
        assert off + n <= SCR_N, (off, n)
        v = scr[:, off:off + n]
        if dt == F32:
            v = v.bitcast(F32)
        return v

    def mg_view(off, n, dt=BF16):
        assert off + n <= NT * T
        v = merged[:, off:off + n]
        if dt == F32:
            v = v.bitcast(F32)
        return v

    mergedv = merged[:, :].rearrange("p (a t) -> p a t", t=T)

    for kt in range(NT):
        simple_load(xT[:, kt, :], xT_d[kt * 128:(kt + 1) * 128, :], ('x', kt))
    simple_load(ident[:], ident_d, 'ident')
    dma('pool', jmatb[:], jmat_d, (), (), 'jmatb')
    simple_load(rotm[:], rot_d, 'rotm')
    simple_load(amask[:], amask_d, 'amask')
    simple_load(cvec[:], cvec_d, 'cvec')
    simple_load(freq[:], freq_d, 'freq')
    simple_load(role[:], role_d, 'role')
    simple_load(fng[:], fng_d, 'fng')
    for l in range(DEPTH):
        simple_load(badat[:, l, :], b_ada_d[l], 'badat')
        simple_load(n1g[:, l, :], n1g_d[l], 'n1g')
        simple_load(n2g[:, l, :], n2g_d[l], 'n2g')
        simple_load(convw[:, l], convw_d[l], 'convw')
        simple_load(qkg[:, l, :], qkg_d[l], 'qkg')
        simple_load(bglu[:, l, :], b_glu_d[l], 'bglu')
        simple_load(sDt[:, l, :], sD_d[l], 'sDt')
    memset(ones_f[:], 1.0, ('ones_f',))
    memset(onesb[:], 1.0)
    memset(epst[:], EPS, ('epst',))
    memset(halfpi[:], math.pi / 2, ('halfpi',))
    cp(identb[:], ident[:], ('ident',), ('identb',))
    ts(rolem1[:], role[:], -1.0, ALU.add, ('role',), ('rolem1',))

    act(scb[:], cvec[:], AF.Silu, ('cvec',), ('scb',))

    def compute_mods_gen(l):
        for cb in range(24):
            wv, wk = wload('A', wblock(w_ada_d[l], cb * 256, 256))
            pb = next_ps(0, 3)
            for t_ in range(2):
                for kt in range(NT):
                    mm(ps[pb][:, t_:t_ + 1], wv[:, kt, t_ * 128:(t_ + 1) * 128], scb[:, kt:kt + 1],
                       kt == 0, kt == NT - 1)
            tt(modv[:, l, 2 * cb:2 * cb + 2], ps[pb][:, 0:2], badat[:, l, 2 * cb:2 * cb + 2], ALU.add)
            yield
        stt(A1[:, l, :], modv[:, l, 8:16], 1.0, n1g[:, l, :], ALU.add, ALU.mult)
        stt(A2[:, l, :], modv[:, l, 32:40], 1.0, n2g[:, l, :], ALU.add, ALU.mult)

    def compute_mods(l):
        for _ in compute_mods_gen(l):
            pass
    compute_mods(0)

    act(pw[:, 0, 0:1], freq[:], AF.Sin, ('freq', 'halfpi'), ('pw',), bias=halfpi[0:64, :])
    act(pw[:, 1, 0:1], freq[:], AF.Sin, ('freq', 'pw'), ('pw',))
    for m in range(6):
        c_m, s_m = pw[:, 0, m:m + 1], pw[:, 1, m:m + 1]
        tt(ttmp[:, 0:1], s_m, s_m, ALU.mult, ('pw',), ('ttmp',))
        stt(pw[:, 0, m + 1:m + 2], c_m, c_m, ttmp[:, 0:1], ALU.mult, ALU.subtract, ('pw', 'ttmp'), ('pw',))
        stt(pw[:, 1, m + 1:m + 2], c_m, 2.0, s_m, ALU.mult, ALU.mult, ('pw',), ('pw',))
    memset(tabC[:, 0:1], 1.0, ('tab',))
    memset(tabS[:, 0:1], 0.0, ('tab',))
    for m in range(6):
        n = 1 << m
        c_m, s_m = pw[:, 0, m:m + 1], pw[:, 1, m:m + 1]
        ts(ttmp[:, 0:n], tabC[:, 0:n], s_m, ALU.mult, ('tab', 'pw'), ('ttmp',))
        stt(tabS[:, n:2 * n], tabS[:, 0:n], c_m, ttmp[:, 0:n], ALU.mult, ALU.add, ('tab', 'ttmp', 'pw'), ('tabS2',))
        ts(ttmp[:, 0:n], tabS[:, 0:n], s_m, ALU.mult, ('tab', 'tabS2', 'pw'), ('ttmp',))
        stt(tabC[:, n:2 * n], tabC[:, 0:n], c_m, ttmp[:, 0:n], ALU.mult, ALU.subtract, ('tab', 'ttmp', 'pw', 'tabS2'), ('tab',))

    def rms_stats(c):
        cs = slice(c * CH, (c + 1) * CH)
        for kt in range(NT):
            sq = sq_t[kt % 2]
            act(sq[:], xT[:, kt, cs], AF.Square, (('x', kt),), (('sq', kt % 2),))
            mm(ps[7][:], ones_f[:], sq[:], kt == 0, kt == NT - 1, (('sq', kt % 2), 'ones_f'), (('ps', 7),))
        act(rstd[:], ps[7][:], AF.Ln, scale=1.0 / D, bias=epst[:])
        act(rstd[:], rstd[:], AF.Exp, scale=-0.5)

    def rmsnorm_to_hT(Acoef, Bshift, l):
        for c in range(NCH):
            cs = slice(c * CH, (c + 1) * CH)
            rms_stats(c)
            for kt in range(NT):
                tf = tmpf[kt % 2]
                stt(tf[:], xT[:, kt, cs], Acoef(kt), rstd[:], ALU.mult, ALU.mult,
                    (('x', kt), 'rstd', ('A1', l), ('A2', l)), (('tmpf', kt % 2),))
                act(hT[:, kt, cs], tf[:], AF.Identity, (('tmpf', kt % 2), ('modv', l)), (('hT', kt, c),),
                    bias=Bshift(kt))

    def proj_hT(wv, wk, col0, M, c, out_ps, pskey):
        cs = slice(c * CH, (c + 1) * CH)
        for kt in range(NT):
            mm(out_ps, wv[:, kt, col0:col0 + M], hT[:, kt, cs], kt == 0, kt == NT - 1,
               (wk, ('hT', kt, c)), (pskey,))

    def merge_contrib(l, first, ysrc, ykeys, nkt, wp_d, gate_col0, chunks):
        pcols = 256 if nkt == 8 else 512
        for nb in range(D // 256):
            gv, gk = wload('A', wblock(w_in_d[l], gate_col0 + nb * 256, 256))
            if (nb * 256) % pcols == 0:
                pv, pk = wload('B', wblock(wp_d[l], nb * 256, pcols))
            poff = (nb * 256) % pcols
            for t_ in range(2):
                dt_ = nb * 2 + t_
                for c in chunks:
                    cs = slice(c * CH, (c + 1) * CH)
                    pg = next_ps(0, 3)
                    proj_hT(gv, gk, t_ * 128, 128, c, ps[pg][:], ('ps', pg))
                    pp = next_ps(3, 3)
                    for kt in range(nkt):
                        mm(ps[pp][:], pv[:, kt, poff + t_ * 128:poff + (t_ + 1) * 128], ysrc(kt, c), kt == 0, kt == nkt - 1,
                           (pk, ykeys(kt, c)), (('ps', pp),))
                    sg = tmpf[2]
                    act(sg[:], ps[pg][:], AF.Sigmoid, (('ps', pg),), (('tmpf', 2),))
                    if first:
                        tt(mergedv[:, dt_, cs], ps[pp][:], sg[:], ALU.mult, (('ps', pp), ('tmpf', 2)), (('mg', dt_, c),))
                    else:
                        tb = tmpb[0]
                        tt(tb[:], ps[pp][:], sg[:], ALU.mult, (('ps', pp), ('tmpf', 2)), (('tmpb', 0),))
                        tt(mergedv[:, dt_, cs], mergedv[:, dt_, cs], tb[:], ALU.add, (('tmpb', 0), ('mg', dt_, c)),
                           (('mg', dt_, c),))

    def head_norm_rope_gen(pq, pqkey, gain, cglob, out_bf, out_keys, kstore=None):
        sq = sqb
        act(sq[0:64, :], pq, AF.Square, (pqkey,), (('sq', 0),))
        yield
        mm(ps[7][0:64, :], onesb[0:64, 0:64], sq[0:64, :], True, True, (('sq', 0), 'ones_f'), (('ps', 7),))
        act(rstd[0:64, :], ps[7][0:64, :], AF.Ln, scale=1.0 / 64, bias=epst[0:64, :])
        act(rstd[0:64, :], rstd[0:64, :], AF.Exp, scale=-0.5)
        qn = tmpf[0]
        stt(qn[0:64, :], pq, gain, rstd[0:64, :], ALU.mult, ALU.mult, (pqkey, 'rstd', 'qkg'), (('tmpf', 0),))
        if kstore is not None:
            kstore(qn)
        yield
        mm(ps[6][0:64, :], rotm[:, :], qn[0:64, :], True, True, (('tmpf', 0), 'rotm'), (('ps', 6),))
        r0 = cglob * 8
        t1 = tmpf[1]
        for (p0, tab_c, tab_s) in ((0, tabC, tabS), (32, tabC, tabS)):
            if p0 == 0:
                cb_ = tab_c[0:32, r0:r0 + 8].unsqueeze(2).to_broadcast([32, 8, 64])
                sb_ = tab_s[0:32, r0:r0 + 8].unsqueeze(2).to_broadcast([32, 8, 64])
            else:
                cb_ = tab_c[32:64, 0:64].unsqueeze(1).to_broadcast([32, 8, 64])
                sb_ = tab_s[32:64, 0:64].unsqueeze(1).to_broadcast([32, 8, 64])
            v3 = lambda ap: ap.rearrange("p (r c) -> p r c", c=64)
            tt(v3(t1[p0:p0 + 32, :]), v3(qn[p0:p0 + 32, :]), cb_, ALU.mult, (('tmpf', 0), 'tab', 'tabS2'), (('tmpf', 1),))
            tt(v3(qn[p0:p0 + 32, :]), v3(ps[6][p0:p0 + 32, :]), sb_, ALU.mult, (('ps', 6), 'tab', 'tabS2', ('tmpf', 0)), (('tmpf', 0),))
        tt(out_bf, t1[0:64, :], qn[0:64, :], ALU.add, (('tmpf', 0), ('tmpf', 1)), out_keys)

    def head_norm_rope(*a, **kw):
        for _ in head_norm_rope_gen(*a, **kw):
            pass

    ssm_state = {'modgen': iter(())}

    def ssm_batch(l, ct, UTv, YSv):
        o = [0]

        def carve(n, dt=BF16, shape=None):
            n_bf = n * (2 if dt == F32 else 1)
            v = scr_view(o[0], n_bf, dt)
            o[0] += n_bf
            return v
        VT = carve(2 * 8 * 256, F32).rearrange("p (r g k) -> p r g k", r=2, g=8)
        M2 = wB[:, :].bitcast(F32).rearrange("p (r g m) -> p r g m", r=2, g=8)
        zoff = o[0]
        Zb = carve(2 * 8 * 128).rearrange("p (r g m) -> p r g m", r=2, g=8)
        N1 = carve(8 * 256).rearrange("p (g q) -> p g q", g=8)
        Kc = carve(8 * 256).rearrange("p (g d m) -> p g d m", g=8, d=2)
        Ug = carve(2 * 256).rearrange("p (d k) -> p d k", d=2)
        M2b = carve(2 * 128).rearrange("p (r m) -> p r m", r=2)
        YTs = scr_view(zoff, 2 * 1024).rearrange("p (d m) -> p d m", d=2)
        prep0 = o[0]
        SM = carve(20 * 8, F32).rearrange("p (a g) -> p a g", g=8)
        PW = carve(2 * 9 * 8, F32).rearrange("p (r k g) -> p r k g", r=2, k=9)
        IP = carve(2 * 9 * 8, F32).rearrange("p (r k g) -> p r k g", r=2, k=9)
        PWd = carve(2 * 8 * 8, F32).rearrange("p (r k g) -> p r k g", r=2, k=8)
        IPd = carve(2 * 8 * 8, F32).rearrange("p (r k g) -> p r k g", r=2, k=8)
        Bt = carve(2 * 128, F32).rearrange("p (r g c) -> p r g c", r=2, g=8)
        Ct = carve(2 * 128, F32).rearrange("p (r g c) -> p r g c", r=2, g=8)
        T1 = carve(4 * 128, F32).rearrange("p (a g c) -> p a g c", a=4, g=8)
        prep1 = o[0]
        SS = carve(16, F32).rearrange("p (r g) -> p r g", r=2)
        AB = carve(32, F32).rearrange("p (a r g) -> p a r g", a=2, r=2)
        SF = carve(128, F32).rearrange("p (s r g) -> p s r g", s=8, r=2)
        SQ = carve(2 * 6 * 8, F32).rearrange("p (r m g) -> p r m g", r=2, m=6)
        EC = carve(16, F32).rearrange("p (r g) -> p r g", r=2)
        EIN = carve(128, F32).rearrange("p (r g s) -> p r g s", r=2, g=8)
        U5 = carve(5 * 16, F32).rearrange("p (a r g) -> p a r g", a=5, r=2)
        end_all = o[0]
        o[0] = prep0
        PT = carve(2 * 8 * 32, F32).rearrange("p (r g j) -> p r g j", r=2, g=8)
        CT = carve(2 * 256, F32).rearrange("p (a g j) -> p a g j", a=2, g=8)
        SSw = carve(128, F32).rearrange("p (r g s) -> p r g s", r=2, g=8)
        T2 = [carve(192, F32).rearrange("p (r g s) -> p r g s", r=3, g=8)]
        W12 = carve(256, F32).rearrange("p (a r g s) -> p a r g s", a=2, r=2, g=8)
        assert o[0] <= prep1, (o[0], prep1)
        o[0] = end_all
        T2.append(carve(192, F32).rearrange("p (r g s) -> p r g s", r=3, g=8))
        K = lambda *a: ('ssm',) + a
        sm = lambda i: SM[:, i, :]
        simple_load(SM[:, 0:3, :], lam_d[l, ct], K('lam'))
        simple_load(Bt[:].rearrange("p r g c -> p r (g c)"), sB_d[l, ct], K('Bt'))
        simple_load(Ct[:].rearrange("p r g c -> p r (g c)"), sC_d[l, ct], K('Ct'))
        simple_load(SS[:].rearrange("p r g -> p (r g)"), h0_d[l, ct], K('SS'))
        LRE, LIM, LDT, DTT, RHO, PHI, MAG, CC_, SS_, AR, AI, U1, U2, U3, U4, DEN, NRE, CRE, CIM, INV = range(20)
        sk = K('sm')
        act(sm(DTT), sm(LDT), AF.Exp, (K('lam'),), (sk,))
        tt(sm(RHO), sm(LRE), sm(DTT), ALU.mult, (sk, K('lam')), (sk,))
        tt(sm(PHI), sm(LIM), sm(DTT), ALU.mult, (sk, K('lam')), (sk,))
        act(sm(MAG), sm(RHO), AF.Exp, (sk,), (sk,), scale=1.0 / 32)
        act(sm(CC_), sm(PHI), AF.Sin, (sk, 'halfpi'), (sk,), scale=1.0 / 32, bias=halfpi[:])
        act(sm(SS_), sm(PHI), AF.Sin, (sk,), (sk,), scale=1.0 / 32)
        tt(sm(AR), sm(MAG), sm(CC_), ALU.mult, (sk,), (sk,))
        tt(sm(AI), sm(MAG), sm(SS_), ALU.mult, (sk,), (sk,))
        for _ in range(5):
            tt(sm(U1), sm(AR), sm(AR), ALU.mult, (sk,), (sk,))
            tt(sm(U2), sm(AI), sm(AI), ALU.mult, (sk,), (sk,))
            stt(sm(U3), sm(AR), 2.0, sm(AI), ALU.mult, ALU.mult, (sk,), (sk,))
            tt(sm(AR), sm(U1), sm(U2), ALU.subtract, (sk,), (sk,))
            cp(sm(AI), sm(U3), (sk,), (sk,))
        tt(sm(U1), sm(LRE), sm(LRE), ALU.mult, (sk, K('lam')), (sk,))
        tt(sm(U2), sm(LIM), sm(LIM), ALU.mult, (sk, K('lam')), (sk,))
        tt(sm(DEN), sm(U1), sm(U2), ALU.add, (sk,), (sk,))
        recip(sm(DEN), sm(DEN), (sk,), (sk,))
        ts(sm(NRE), sm(AR), -1.0, ALU.add, (sk,), (sk,))
        tt(sm(U1), sm(NRE), sm(LRE), ALU.mult, (sk,), (sk,))
        tt(sm(U2), sm(AI), sm(LIM), ALU.mult, (sk,), (sk,))
        tt(sm(U3), sm(U1), sm(U2), ALU.add, (sk,), (sk,))
        tt(sm(CRE), sm(U3), sm(DEN), ALU.mult, (sk,), (sk,))
        tt(sm(U1), sm(AI), sm(LRE), ALU.mult, (sk,), (sk,))
        tt(sm(U2), sm(NRE), sm(LIM), ALU.mult, (sk,), (sk,))
        tt(sm(U3), sm(U1), sm(U2), ALU.subtract, (sk,), (sk,))
        tt(sm(CIM), sm(U3), sm(DEN), ALU.mult, (sk,), (sk,))
        pk_ = K('pw')
        memset(PW[:, 0, 0, :], 1.0, (pk_,))
        memset(PW[:, 1, 0, :], 0.0, (pk_,))
        memset(IP[:, 0, 0, :], 1.0, (pk_,))
        memset(IP[:, 1, 0, :], 0.0, (pk_,))
        cp(PW[:, 0, 1, :], sm(AR), (sk,), (pk_,))
        cp(PW[:, 1, 1, :], sm(AI), (sk,), (pk_,))
        tt(sm(U1), sm(AR), sm(AR), ALU.mult, (sk,), (sk,))
        tt(sm(U2), sm(AI), sm(AI), ALU.mult, (sk,), (sk,))
        tt(sm(INV), sm(U1), sm(U2), ALU.add, (sk,), (sk,))
        recip(sm(INV), sm(INV), (sk,), (sk,))
        tt(IP[:, 0, 1, :], sm(AR), sm(INV), ALU.mult, (sk,), (pk_,))
        stt(IP[:, 1, 1, :], sm(AI), -1.0, sm(INV), ALU.mult, ALU.mult, (sk,), (pk_,))

        def cmul(dst_r, dst_i, xr, xi, yr, yi, rkeys, wkeys):
            tt(sm(U1), xr, yr, ALU.mult, rkeys + (sk,), (sk,))
            tt(sm(U2), xi, yi, ALU.mult, rkeys + (sk,), (sk,))
            tt(sm(U3), xr, yi, ALU.mult, rkeys + (sk,), (sk,))
            tt(sm(U4), xi, yr, ALU.mult, rkeys + (sk,), (sk,))
            tt(dst_r, sm(U1), sm(U2), ALU.subtract, (sk,), wkeys)
            tt(dst_i, sm(U3), sm(U4), ALU.add, (sk,), wkeys)
        for k in range(1, 8):
            cmul(PW[:, 0, k + 1, :], PW[:, 1, k + 1, :], PW[:, 0, k, :], PW[:, 1, k, :], PW[:, 0, 1, :], PW[:, 1, 1, :], (pk_,), (pk_,))
            cmul(IP[:, 0, k + 1, :], IP[:, 1, k + 1, :], IP[:, 0, k, :], IP[:, 1, k, :], IP[:, 0, 1, :], IP[:, 1, 1, :], (pk_,), (pk_,))
        dk_ = K('pwd')
        cp(PWd[0:64], PW[0:64, :, 1:9, :], (pk_,), (dk_,))
        cp(IPd[0:64], IP[0:64, :, 1:9, :], (pk_,), (dk_,))
        for j in range(8):
            cp(PWd[64:128, :, j, :], PW[64:128, :, 8 - j, :], (pk_,), (dk_,))
            cp(IPd[64:128, :, j, :], IP[64:128, :, 8 - j, :], (pk_,), (dk_,))
        cp(AB[:, 0, 0, :], PW[:, 0, 8, :], (pk_,), (K('AB'),))
        cp(AB[:, 0, 1, :], PW[:, 0, 8, :], (pk_,), (K('AB'),))
        ts(AB[:, 1, 0, :], PW[:, 1, 8, :], -1.0, ALU.mult, (pk_,), (K('AB'),))
        cp(AB[:, 1, 1, :], PW[:, 1, 8, :], (pk_,), (K('AB'),))
        bc = lambda ap: ap.unsqueeze(2).to_broadcast([128, 8, 16])
        tk = K('T1')
        tt(T1[:, 0], Bt[:, 0], bc(sm(CRE)), ALU.mult, (K('Bt'), sk), (tk,))
        tt(T1[:, 1], Bt[:, 1], bc(sm(CIM)), ALU.mult, (K('Bt'), sk), (tk,))
        tt(T1[:, 2], Bt[:, 0], bc(sm(CIM)), ALU.mult, (K('Bt'), sk), (tk,))
        tt(T1[:, 3], Bt[:, 1], bc(sm(CRE)), ALU.mult, (K('Bt'), sk), (tk,))
        tt(Bt[:, 0], T1[:, 0], T1[:, 1], ALU.subtract, (tk,), (K('Bt'),))
        tt(Bt[:, 1], T1[:, 2], T1[:, 3], ALU.add, (tk,), (K('Bt'),))
        Zv = lambda r: Zb[:, r].rearrange("p g (i c) -> p g i c", i=8)
        M2v = lambda r: M2[:, r].rearrange("p g (j c) -> p g j c", j=8)
        def z_chain():
            for i in range(8):
                pr_, pi_ = bc(IPd[:, 0, i, :]), bc(IPd[:, 1, i, :])
                tt(T1[:, 0], Bt[:, 0], pr_, ALU.mult)
                yield
                tt(T1[:, 1], Bt[:, 1], pi_, ALU.mult)
                yield
                tt(Zv(0)[:, :, i, :], T1[:, 0], T1[:, 1], ALU.subtract)
                yield
                tt(T1[:, 0], Bt[:, 0], pi_, ALU.mult)
                yield
                tt(T1[:, 1], Bt[:, 1], pr_, ALU.mult)
                yield
                tt(Zv(1)[:, :, i, :], T1[:, 0], T1[:, 1], ALU.add)
                yield

        def m2_chain():
            for i in range(8):
                pr_, pi_ = bc(PWd[:, 0, i, :]), bc(PWd[:, 1, i, :])
                tt(T1[:, 2], Ct[:, 0], pr_, ALU.mult)
                yield
                tt(T1[:, 3], Ct[:, 1], pi_, ALU.mult)
                yield
                tt(M2v(0)[:, :, i, :], T1[:, 2], T1[:, 3], ALU.subtract)
                yield
                tt(T1[:, 2], Ct[:, 0], pi_, ALU.mult)
                yield
                tt(T1[:, 3], Ct[:, 1], pr_, ALU.mult)
                yield
                stt(M2v(1)[:, :, i, :], T1[:, 2], -1.0, T1[:, 3], ALU.mult, ALU.subtract)
                yield
        za, mb = z_chain(), m2_chain()
        for _ in range(48):
            next(za, None)
            next(mb, None)
        if SSM_SUB < 2:
            return
        for g in range(8):
            pn = [0 + (g % 2), 2 + (g % 2)]
            for d_ in range(2):
                for r in range(2):
                    mm(ps[pn[d_]][:, r * 64:(r + 1) * 64], Zb[d_ * 64:(d_ + 1) * 64, r, g, :],
                       identb[d_ * 64:(d_ + 1) * 64, d_ * 64:(d_ + 1) * 64], True, True)
                cp(N1[:, g, d_ * 128:(d_ + 1) * 128], ps[pn[d_]][:, 0:128], eng='act')
            cp(M2b[:, 0, :], M2[:, 0, g, :])
            cp(M2b[:, 1, :], M2[:, 1, g, :])
            pk2 = [4 + (g % 2), 6 + (g % 2)]
            for d_ in range(2):
                sl = slice(d_ * 64, (d_ + 1) * 64)
                for r in range(2):
                    mm(ps[pk2[d_]][:, 0:128], Zb[sl, r, g, :], M2b[sl, r, :], r == 0, r == 1)
            for d_ in range(2):
                stt(Kc[:, g, d_, :], ps[pk2[d_]][:, 0:128], -1.0, amask[:, d_, :], ALU.mult, ALU.mult)
            gg = ct * 8 + g
            stt(Kc[:, g, 0, :], ident[:], sDt[:, l, gg:gg + 1], Kc[:, g, 0, :], ALU.mult, ALU.add,
                ('ident', 'sDt', K('Kc', g)), (K('Kc', g),))

        def make_Ug(g):
            pu = next_ps(0, 3)
            for blk in range(2):
                mm(ps[pu][:, blk * 128:(blk + 1) * 128], UTv[:, blk, ct * 8 + g, :], identb[:], True, True,
                   (('UT', blk), 'identb'), (('ps', pu),))
                mm(ps[pu][:, 256 + blk * 128:256 + (blk + 1) * 128], UTv[:, 1 - blk, ct * 8 + g, :], jmatb[:], True, True,
                   (('UT', 1 - blk), 'jmatb'), (('ps', pu),))
            cp(Ug[:].rearrange("p d k -> p (d k)"), ps[pu][:], (('ps', pu),), (K('Ug'),), eng='act')
        if SSM_SUB < 3:
            return
        for g in range(8):
            make_Ug(g)
            pv_ = next_ps(3, 3)
            for d_ in range(2):
                for r in range(2):
                    q0 = (d_ * 2 + r) * 64
                    mm(ps[pv_][d_ * 64:(d_ + 1) * 64, r * 256:(r + 1) * 256], N1[:, g, q0:q0 + 64], Ug[:, d_, :], True, True,
                       (K('N1', g), K('Ug')), (('ps', pv_),))
            cp(VT[:, :, g, :], ps[pv_][:].rearrange("p (r k) -> p r k", r=2), (('ps', pv_),), (K('VT'),))
        if SSM_SUB < 4:
            return
        VTs = VT.rearrange("p r g (s j) -> p r g s j", s=8)
        tk_ = K('VT')
        bc4 = lambda ap: ap.unsqueeze(3).to_broadcast([128, 2, 8, 8])
        def cmul2(dst, x, y, tmp):
            tt(tmp[:, 0], x[:, 0], y[:, 0], ALU.mult)
            tt(tmp[:, 1], x[:, 1], y[:, 1], ALU.mult)
            tt(tmp[:, 2], x[:, 0], y[:, 1], ALU.mult)
            tt(tmp[:, 3], x[:, 1], y[:, 0], ALU.mult)
            tt(dst[:, 0], tmp[:, 0], tmp[:, 1], ALU.subtract)
            tt(dst[:, 1], tmp[:, 2], tmp[:, 3], ALU.add)

        def tab_gen():
            cp(SQ[:, 0, 0, :], AB[:, 0, 0, :])
            cp(SQ[:, 1, 0, :], AB[:, 1, 1, :])
            yield
            for m in range(5):
                cmul2(SQ[:, :, m + 1, :], SQ[:, :, m, :], SQ[:, :, m, :], U5[:, 0:4, 0, :])
                yield
            memset(PT[:, 0, :, 0:1], 1.0)
            memset(PT[:, 1, :, 0:1], 0.0)
            for m in range(5):
                n = 1 << m
                yb = lambda r: SQ[:, r, m, :].unsqueeze(2).to_broadcast([128, 8, n])
                tt(CT[:, 0, :, 0:n], PT[:, 0, :, 0:n], yb(0), ALU.mult)
                tt(CT[:, 1, :, 0:n], PT[:, 1, :, 0:n], yb(1), ALU.mult)
                tt(PT[:, 0, :, n:2 * n], CT[:, 0, :, 0:n], CT[:, 1, :, 0:n], ALU.subtract)
                yield
                tt(CT[:, 0, :, 0:n], PT[:, 0, :, 0:n], yb(1), ALU.mult)
                tt(CT[:, 1, :, 0:n], PT[:, 1, :, 0:n], yb(0), ALU.mult)
                tt(PT[:, 1, :, n:2 * n], CT[:, 0, :, 0:n], CT[:, 1, :, 0:n], ALU.add)
                yield
        tg = tab_gen()
        for _ in range(6):
            next(ssm_state['modgen'], None)
        memset(SSw[:], 0.0)
        for j in range(32):
            t2 = T2[j % 2]
            tt(t2[:, 0:2], SSw[:], VTs[:, :, :, :, j], ALU.add)
            next(tg, None)
            tt(t2[:, 2], SSw[:, 0], VTs[:, 0, :, :, j], ALU.add)
            tt(W12[:, 0], t2[:, 0:2], bc4(AB[:, 0]), ALU.mult)
            tt(W12[:, 1], t2[:, 1:3], bc4(AB[:, 1]), ALU.mult)
            tt(SSw[:], W12[:, 0], W12[:, 1], ALU.add)
            cp(VTs[:, :, :, :, j], t2[:, 0:2], eng='act')
        for _ in tg:
            pass
        ts(EC[:], SS[:], role[:, 0:1], ALU.mult)
        for sg_ in range(8):
            cp(EIN[:, :, :, sg_], EC[:])
            cmul2(U5[:, 4], SQ[:, :, 5, :], EC[:], U5[:, 0:4, 0, :])
            tt(SF[:, sg_], SSw[:, :, :, sg_], U5[:, 4], ALU.add)
            ts(EC[:], SF[:, sg_], role[:, 0:1], ALU.mult)
        store(sout_d[l, ct], SF[:].rearrange("p s r g -> p (s r g)"), (), K('SF'))
        for sg_ in range(8):
            eb = lambda r: EIN[:, r, :, sg_].unsqueeze(2).to_broadcast([128, 8, 32])
            tt(CT[:, 0], PT[:, 0], eb(0), ALU.mult)
            tt(CT[:, 1], PT[:, 1], eb(1), ALU.mult)
            tt(CT[:, 0], CT[:, 0], CT[:, 1], ALU.subtract)
            tt(VTs[:, 0, :, sg_, :], VTs[:, 0, :, sg_, :], CT[:, 0], ALU.add)
            tt(CT[:, 0], PT[:, 0], eb(1), ALU.mult)
            tt(CT[:, 1], PT[:, 1], eb(0), ALU.mult)
            tt(CT[:, 0], CT[:, 0], CT[:, 1], ALU.add)
            tt(VTs[:, 1, :, sg_, :], VTs[:, 1, :, sg_, :], CT[:, 0], ALU.add)
        if SSM_SUB < 5:
            return
        for tb in range(2):
            pyf = [4, 5]
            pyb = [6, 7]
            for g in range(8):
                make_Ug(g)
                bank, col = g // 4, (g % 4) * 128
                for (d_, pys, kb_) in ((0, pyf, tb), (1, pyb, 1 - tb)):
                    sl = slice(d_ * 64, (d_ + 1) * 64)
                    ksl = slice(kb_ * 128, (kb_ + 1) * 128)
                    o_ = ps[pys[bank]][:, col:col + 128]
                    mm(o_, VT[sl, 0, g, ksl], M2[sl, 0, g, :], True, False, (tk_, ('w', 'B', 0), ('w', 'B', 1)), (('ps', pys[bank]),))
                    mm(o_, VT[sl, 1, g, ksl], M2[sl, 1, g, :], False, False, (tk_, ('w', 'B', 0), ('w', 'B', 1)), (('ps', pys[bank]),))
                    mm(o_, Ug[:, d_, ksl], Kc[:, g, d_, :], False, True, (K('Ug'), K('Kc', g)), (('ps', pys[bank]),))
            for bank in range(2):
                for (d_, pys, eng_) in ((0, pyf, 'act'), (1, pyb, 'dve')):
                    cp(YTs[:, d_, :].rearrange("p (j g c) -> p g j c", j=8, g=8)[:, bank * 4:(bank + 1) * 4, :, :],
                       ps[pys[bank]][:].rearrange("p (g j c) -> p g j c", g=4, j=8), (('ps', pys[bank]),), (K('Z'),), eng=eng_)
            for jh in range(2):
                pt = next_ps(0, 3)
                for jj in range(4):
                    j = jh * 4 + jj
                    o_ = ps[pt][:, jj * 128:(jj + 1) * 128]
                    ltf = YTs[:, 0, j * 128:(j + 1) * 128]
                    ltb = YTs[:, 1, j * 128:(j + 1) * 128]
                    mm(o_, ltf, identb[:], True, False, (K('Z'), 'identb'), (('ps', pt),))
                    mm(o_, ltb, jmatb[:], False, True, (K('Z'), 'jmatb'), (('ps', pt),))
                y = ps[pt][:]
                y2 = tmpf[0]
                act(y2[:], y, AF.Square, (('ps', pt),), (('tmpf', 0),))
                ts(y2[:], y2[:], 0.044715, ALU.mult, (('tmpf', 0),), (('tmpf', 0),), s2=1.0, op1=ALU.add)
                tt(y2[:], y2[:], y, ALU.mult, (('tmpf', 0), ('ps', pt)), (('tmpf', 0),))
                sg = tmpf[1]
                act(sg[:], y2[:], AF.Sigmoid, (('tmpf', 0),), (('tmpf', 1),), scale=1.5957691216057308)
                dest = YSv[:, ct, tb * 1024:(tb + 1) * 1024].rearrange("p (k j) -> p j k", j=8)[:, jh * 4:(jh + 1) * 4, :]
                v3 = lambda ap: ap.rearrange("p (j k) -> p j k", j=4)
                tt(dest, v3(y), v3(sg[:]), ALU.mult, (('ps', pt), ('tmpf', 1)), tuple(('YH', ct, 2 * tb + cc) for cc in range(2)))

    def ssm_post(l, YH, YG):
        gv, gk = wload('B', wblock(w_glu_d[l], 0, 512))
        for nt_ in range(4):
            for c in range(NCH):
                cs = slice(c * CH, (c + 1) * CH)
                p_ = next_ps(0, 3)
                for kt in range(4):
                    mm(ps[p_][:], gv[:, kt, nt_ * 128:(nt_ + 1) * 128], YH[:, kt, cs], kt == 0, kt == 3,
                       (gk, ('YH', kt, c)), (('ps', p_),))
                sg = tmpf[2]
                act(sg[:], ps[p_][:], AF.Sigmoid, (('ps', p_), 'bglu'), (('tmpf', 2),), bias=bglu[:, l, nt_:nt_ + 1])
                tt(YG[:, nt_, cs], YH[:, nt_, cs], sg[:], ALU.mult, (('YH', nt_, c), ('tmpf', 2)), (('YG', nt_, c),))

    def attention(l, qh):
        YA = scr_view(0, 8 * 1024).rearrange("p (a t) -> p a t", t=1024)
        qaug = [scr_view(8192 + i * 1024, 1024) for i in range(2)]
        kaug = scr_view(8192 + 2048, NK * 128)
        vaug = scr_view(8192 + 2048 + NK * 128, NK * 256).rearrange("p (k a m) -> p k a m", k=NK, a=2)
        vst = scr_view(8192 + 2048 + NK * 128 + NK * 256, 2048, F32).rearrange("p (b m) -> p b m", m=64)
        chunks = (2 * qh, 2 * qh + 1)
        AK = lambda *a: ('att',) + a
        for i in range(2):
            dma('pool', qaug[i][64:74, :], qmask_d[:, qh * 1024:(qh + 1) * 1024], (), (AK('qm', i),), AK('qm', i))
        dma('pool', kaug[64:74, :], kmask_d, (), (AK('km'),), AK('km'))
        memset(vaug[:, :, 0, 64:128], 1.0, (AK('v1'),))
        memset(vaug[:, :, 1, 0:64], 1.0, (AK('v1'),))
        state = {'n': 0, 'pend': None, 'wq': None}

        def load_wq(hq):
            if hq % 4 == 0:
                state['wq'] = wload('B', wblock(w_in_d[l], 1536 + 64 * hq, 256))

        def prep_q_gen(hq, ci):
            wqv, wqk = state['wq']
            qa = qaug[hq % 2]
            c = chunks[ci]
            proj_hT(wqv, wqk, (hq % 4) * 64, 64, c, ps[5][0:64, :], ('ps', 5))
            yield from head_norm_rope_gen(ps[5][0:64, :], ('ps', 5), qkg[:, l, 0:1], c, qa[0:64, ci * CH:(ci + 1) * CH], ())

        def prep_q(hq):
            load_wq(hq)
            for ci in range(2):
                for _ in prep_q_gen(hq, ci):
                    pass

        def attn_loop(hq, ci, po, side=None, epi=None):
            par = hq % 2
            qa = qaug[hq % 2]
            LOOK = 2

            def emit_pv(kb):
                mm(ps[po][:], vaug[:, kb, par, :], tmpb[kb % 4][:], kb == 0, kb == NK - 1)
            for kb in range(NK):
                if side is not None and kb in (0, 9, 17):
                    next(side, None)
                if epi is not None and kb in (2, 10, 16):
                    next(epi, None)
                pss = kb % 3
                mm(ps[pss][:], kaug[0:74, kb * 128:(kb + 1) * 128], qa[0:74, ci * CH:(ci + 1) * CH], True, True)
                act(tmpb[kb % 4][:], ps[pss][:], AF.Exp, scale=0.125)
                if kb >= LOOK:
                    emit_pv(kb - LOOK)
            for kb in range(NK - LOOK, NK):
                emit_pv(kb)
            if side is not None:
                for _ in side:
                    pass
            if epi is not None:
                for _ in epi:
                    pass

        def epilogue_gen(hq, ci, po):
            par = hq % 2
            nsl = slice(par * 64, par * 64 + 64)
            dp = (1 - par) * 64
            rd = tmpf[2]
            recip(rd[dp:dp + 1, :], ps[po][dp:dp + 1, :])
            yield
            pbc = 5 if par == 0 else 6
            mm(ps[pbc][nsl, :], ones_f[dp:dp + 1, 0:64], rd[dp:dp + 1, :], True, True)
            bc_ = sq_t[1]
            cp(bc_[nsl, :], ps[pbc][nsl, :], eng='act')
            yield
            tt(YA[nsl, hq // 2, ci * CH:(ci + 1) * CH], ps[po][nsl, :], bc_[nsl, :], ALU.mult)

        def epilogue(hq, ci, po):
            for _ in epilogue_gen(hq, ci, po):
                pass

        for g in range(4):
            if state['pend'] is not None:
                epilogue(*state['pend'])
                state['pend'] = None
            wv, wk = wload('A', wblock(w_in_d[l], 2560 + 64 * g, 64))
            for c in range(NCH):
                cs = slice(c * CH, (c + 1) * CH)
                proj_hT(wv, wk, 0, 64, c, ps[5][0:64, :], ('ps', 5))
                kst = None
                if qh == 0:
                    def kst(qn, c=c, g=g):
                        store(kout_d[l, g, :, c * CH:(c + 1) * CH], qn[0:64, :], (('tmpf', 0),), ('tmpf', 0))
                head_norm_rope(ps[5][0:64, :], ('ps', 5), qkg[:, l, 1:2], c, kaug[0:64, cs], (AK('k', c),), kstore=kst)
            dma('pool', kaug[0:64, T:T + 256], ckT_d[l, g], (), (AK('kc'),), AK('kc'))
            wv, wk = wload('A', wblock(w_in_d[l], 2816 + 64 * g, 64))
            for tq in range(2):
                pv_ = next_ps(3, 2)
                for t8 in range(8):
                    tb = tq * 8 + t8
                    for kt in range(NT):
                        mm(ps[pv_][:, t8 * 64:(t8 + 1) * 64], hT[:, kt, tb * 128:(tb + 1) * 128], wv[:, kt, :],
                           kt == 0, kt == NT - 1)
                src8 = ps[pv_][:, 0:512].rearrange("p (b m) -> p b m", m=64)
                cp(vaug[:, tq * 8:(tq + 1) * 8, 0, 0:64], src8, eng='act')
                cp(vaug[:, tq * 8:(tq + 1) * 8, 1, 64:128], src8)
                if qh == 0:
                    cp(vst[:, tq * 8:(tq + 1) * 8, :], src8)
            if qh == 0:
                store(vout_d[l, g], vst[:].rearrange("p b m -> p (b m)"), (AK('vst'),), AK('vst'))
            for blk in range(2):
                dma('pool', vaug[:, 16 + blk, 0, 0:64], cv_d[l, :, blk, g * 64:(g + 1) * 64], (), (AK('vc'),), AK('vc'))
                dma('pool', vaug[:, 16 + blk, 1, 64:128], cv_d[l, :, blk, g * 64:(g + 1) * 64], (), (AK('vc'),), AK('vc'))
            for r_ in range(4):
                hq = 4 * g + r_
                if hq == 0:
                    prep_q(0)
                if hq + 1 < 16:
                    load_wq(hq + 1)
                for ci in range(2):
                    po = 3 + (state['n'] % 2)
                    state['n'] += 1
                    side = prep_q_gen(hq + 1, ci) if hq + 1 < 16 else None
                    epi = epilogue_gen(*state['pend']) if state['pend'] is not None else None
                    attn_loop(hq, ci, po, side, epi)
                    state['pend'] = (hq, ci, po)
        if state['pend'] is not None:
            epilogue(*state['pend'])
            state['pend'] = None
        merge_contrib(l, False, lambda kt, c: YA[:, kt, (c - 2 * qh) * CH:(c - 2 * qh + 1) * CH],
                      lambda kt, c: AK('ya', kt, c - 2 * qh), 8, wpa_d, 4608, chunks)

    for l in range(DEPTH):
        sh1 = lambda kt, l=l: modv[:, l, kt:kt + 1]
        a1 = lambda kt, l=l: A1[:, l, kt:kt + 1]
        if STAGE >= 1:
            rmsnorm_to_hT(a1, sh1, l)

        if ENABLE_SSM and STAGE >= 2:
            YH = mg_view(0, 4 * T).rearrange("p (a t) -> p a t", t=T)
            UT = mg_view(4 * T, 2 * 4096).rearrange("p (b f) -> p b f", f=4096)
            wv, wk = wload('A', wblock(w_in_d[l], 3072, 256))
            wv2, wk2 = wload('A', wblock(w_in_d[l], 3328, 256))
            for blk in range(2):
                for i in range(8):
                    p_ = next_ps(0, 3)
                    for hf, (wvv, wkk) in enumerate(((wv, wk), (wv2, wk2))):
                        for kt in range(NT):
                            lt = hT[:, kt, blk * 1024 + i:blk * 1024 + i + 1017:8]
                            mm(ps[p_][:, hf * 256:(hf + 1) * 256], lt, wvv[:, kt, :], kt == 0, kt == NT - 1,
                               (wkk,) + tuple(('hT', kt, c_) for c_ in (2 * blk, 2 * blk + 1)), (('ps', p_),))
                    cp(UT[:, blk, :].rearrange("p (g i c) -> p i g c", g=32, i=8)[:, i, :, :],
                       ps[p_][:].rearrange("p (g c) -> p g c", g=32), (('ps', p_),), (('UT', blk),), eng='act')
            UTv = UT.rearrange("p b (g m) -> p b g m", g=32)
            ssm_state['modgen'] = compute_mods_gen(l + 1) if l + 1 < DEPTH else iter(())
            for ct in range(4 if SSM_SUB >= 1 else 0):
                ssm_batch(l, ct, UTv, YH)
            for _ in ssm_state['modgen']:
                pass
            if SSM_SUB < 6:
                continue
            YG = scr_view(0, 4 * T).rearrange("p (a t) -> p a t", t=T)
            ssm_post(l, YH, YG)
            merge_contrib(l, True, lambda kt, c: YG[:, kt, c * CH:(c + 1) * CH], lambda kt, c: ('YG', kt, c), 4, wps_d, 5632,
                          range(NCH))

        if STAGE < 3:
            continue
        CC = scr_view(0, 4 * T).rearrange("p (a t) -> p a t", t=T)
        CB = scr_view(4 * T, 4 * T).rearrange("p (a t) -> p a t", t=T)
        for stage, col0 in ((0, 512), (1, 1024), (2, 0)):
            for hb in range(2):
                wv, wk = wload('A', wblock(w_in_d[l], col0 + hb * 256, 256))
                for t2_ in range(2):
                    t_ = hb * 2 + t2_
                    for c in range(NCH):
                        cs = slice(c * CH, (c + 1) * CH)
                        p_ = next_ps(0, 3)
                        proj_hT(wv, wk, t2_ * 128, 128, c, ps[p_][:], ('ps', p_))
                        if stage == 0:
                            cp(CC[:, t_, cs], ps[p_][:], (('ps', p_),), (('CC', t_, c),), eng='act')
                        elif stage == 1:
                            tt(CC[:, t_, cs], ps[p_][:], CC[:, t_, cs], ALU.mult, (('ps', p_), ('CC', t_, c)), (('CC', t_, c),))
                        else:
                            cp(CB[:, t_, cs], ps[p_][:], (('ps', p_),), (('CB', t_, c),), eng='act')
        for t_ in range(4):
            w0, w1, w2, bb = (convw[:, l, t_, i:i + 1] for i in range(4))
            tt(convx[:, t_, 0:1], w0, rolem1[:], ALU.mult, ('convw', 'rolem1'), (('convx', t_),))
            tt(convx[:, t_, 1:2], w2, rolem1[:], ALU.mult, ('convw', 'rolem1'), (('convx', t_),))
            for c in range(NCH):
                lo, hi = c * CH, (c + 1) * CH
                acc = tmpf[c % 2]
                ak = ('tmpf', c % 2)
                rk = [('CC', t_, cc_) for cc_ in range(max(0, c - 1), min(NCH, c + 2))]
                act(acc[:], CC[:, t_, lo:hi], AF.Identity, rk + ['convw'], (ak,), scale=w1, bias=bb)
                l0 = max(lo - 1, 0)
                o0 = l0 + 1 - lo
                stt(acc[:, o0:CH], CC[:, t_, l0:hi - 1], w0, acc[:, o0:CH], ALU.mult, ALU.add, rk + ['convw', ak], (ak,))
                h1 = min(hi + 1, T)
                n1 = h1 - (lo + 1)
                stt(acc[:, 0:n1], CC[:, t_, lo + 1:h1], w2, acc[:, 0:n1], ALU.mult, ALU.add, rk + ['convw', ak], (ak,))
                lcols = [b for b in (lo, lo + 256) if b > 0]
                for b in lcols:
                    stt(acc[:, b - lo:b - lo + 1], CC[:, t_, b - 1:b], convx[:, t_, 0:1], acc[:, b - lo:b - lo + 1],
                        ALU.mult, ALU.add, rk + [('convx', t_), ak], (ak,))
                rcols = [b for b in (lo + 255, lo + 511) if b < T - 1]
                for b in rcols:
                    stt(acc[:, b - lo:b - lo + 1], CC[:, t_, b + 1:b + 2], convx[:, t_, 1:2], acc[:, b - lo:b - lo + 1],
                        ALU.mult, ALU.add, rk + [('convx', t_), ak], (ak,))
                tt(CB[:, t_, lo:hi], acc[:], CB[:, t_, lo:hi], ALU.mult, (ak, ('CB', t_, c)), (('CB', t_, c),))
        merge_contrib(l, not ENABLE_SSM, lambda kt, c: CB[:, kt, c * CH:(c + 1) * CH], lambda kt, c: ('CB', kt, c), 4, wpc_d, 3584,
                      range(NCH))

        if ENABLE_ATTN and STAGE >= 4:
            for qh in range(2):
                attention(l, qh)

        if STAGE < 5:
            continue
        for nb in range(4):
            wv, wk = wload('A', wblock(w_out_d[l], nb * 256, 256))
            for t_ in range(2):
                dt_ = nb * 2 + t_
                for c in range(NCH):
                    cs = slice(c * CH, (c + 1) * CH)
                    p_ = next_ps(0, 3)
                    for kt in range(NT):
                        mm(ps[p_][:], wv[:, kt, t_ * 128:(t_ + 1) * 128], mergedv[:, kt, cs], kt == 0, kt == NT - 1,
                           (wk, ('mg', kt, c)), (('ps', p_),))
                    stt(xT[:, dt_, cs], ps[p_][:], modv[:, l, 16 + dt_:17 + dt_], xT[:, dt_, cs], ALU.mult, ALU.add,
                        (('ps', p_), ('modv', l), ('x', dt_)), (('x', dt_),))

        if STAGE < 6:
            continue
        sh2 = lambda kt, l=l: modv[:, l, 24 + kt:25 + kt]
        a2 = lambda kt, l=l: A2[:, l, kt:kt + 1]
        rmsnorm_to_hT(a2, sh2, l)
        ACT_m = mg_view(0, 16 * 1024).rearrange("p (f t) -> p f t", t=1024)
        ACT_s = scr_view(0, 6 * 1024).rearrange("p (f t) -> p f t", t=1024)

        def act_tile(ft, lo, hi):
            return ACT_m[:, ft, lo:hi] if ft < 16 else ACT_s[:, ft - 16, lo:hi]
        modgen = iter(())
        for half in range(2):
            for fb in range(NFT // 2):
                next(modgen, None)
                gv, gk = wload('A', wblock(wfg_d[l], fb * 256, 256))
                uv, uk = wload('B', wblock(wfu_d[l], fb * 256, 256))
                for t_ in range(2):
                    ft = fb * 2 + t_
                    for c2 in range(2):
                        c = half * 2 + c2
                        pg = next_ps(0, 3)
                        proj_hT(gv, gk, t_ * 128, 128, c, ps[pg][:], ('ps', pg))
                        pu = next_ps(3, 3)
                        proj_hT(uv, uk, t_ * 128, 128, c, ps[pu][:], ('ps', pu))
                        sg = tmpb[3]
                        act(sg[:], ps[pg][:], AF.Silu, (('ps', pg),), (('tmpb', 3),))
                        tt(act_tile(ft, c2 * CH, (c2 + 1) * CH), ps[pu][:], sg[:], ALU.mult, (('ps', pu), ('tmpb', 3)),
                           (('ACT', ft, c2),))
            for dt_ in range(NT):
                pool_ = 'A' if dt_ % 2 == 0 else 'B'
                dv, dk = wload(pool_, wblock(wfd_d[l], dt_ * 128, 128), full=True)
                dk2 = ('w', pool_, 1)
                for c2 in range(2):
                    c = half * 2 + c2
                    cs = slice(c * CH, (c + 1) * CH)
                    p_ = next_ps(0, 3)
                    for ft in range(NFT):
                        mm(ps[p_][:], dv[:, ft, :], act_tile(ft, c2 * CH, (c2 + 1) * CH), ft == 0, ft == NFT - 1,
                           (dk, dk2, ('ACT', ft, c2)), (('ps', p_),))
                    stt(xT[:, dt_, cs], ps[p_][:], modv[:, l, 40 + dt_:41 + dt_], xT[:, dt_, cs], ALU.mult, ALU.add,
                        (('ps', p_), ('modv', l), ('x', dt_)), (('x', dt_),))
        for _ in modgen:
            pass

    for c in range(NCH):
        cs = slice(c * CH, (c + 1) * CH)
        rms_stats(c)
        for kt in range(NT):
            tf = tmpf[kt % 3]
            stt(tf[:], xT[:, kt, cs], fng[:, kt:kt + 1], rstd[:], ALU.mult, ALU.mult, (('x', kt), 'rstd', 'fng'),
                (('tmpf', kt % 3),))
            store(yT_d[kt * 128:(kt + 1) * 128, cs], tf[:], (('tmpf', kt % 3),), ('tmpf', kt % 3))

    S.emit(es, store_keys)
    _STATS['nops'] = len(S.ops)
    es.close()
    return nc, es


_PROG = {}


def _consts():
    ident = np.eye(128, dtype=np.float32)
    rot = np.zeros((64, 64), np.float32)
    for b0 in (0, 32):
        for j in range(16):
            rot[b0 + 16 + j, b0 + j] = -1.0
            rot[b0 + j, b0 + 16 + j] = 1.0
    ii = np.arange(128) // 16
    amask = np.zeros((128, 2, 128), np.float32)
    amask[:, 0, :] = (ii[:, None] > ii[None, :])
    amask[:, 1, :] = (ii[:, None] < ii[None, :])
    jmat = np.ascontiguousarray(np.eye(128, dtype=np.float32)[::-1])
    return ident, rot, amask, jmat


def _core_inputs(core, inp, shared):
    f32 = np.float32
    prompt = core < 4
    d = dict(shared)
    if prompt:
        x = inp['x_prompt'][core * 8:(core + 1) * 8].reshape(T, D)
        cv = inp['c_ctx']
    else:
        x = inp['x_sample'][core - 4]
        cv = inp['c'][core - 4]
    d['xT'] = np.ascontiguousarray(x.T)
    d['cvec'] = np.ascontiguousarray(cv.reshape(NT, 128).T)
    j = np.arange(64) % 16
    inv = (10000.0 ** (-(j.astype(np.float64)) / 16.0)).astype(f32)
    d['freq'] = (np.zeros((64, 1), f32) if prompt else inv.reshape(64, 1).copy())
    d['role'] = np.full((128, 1), 0.0 if prompt else 1.0, f32)
    seg_q = np.arange(T) // 256
    qm = np.zeros((10, T), f32)
    km = np.zeros((10, NK * 128), f32)
    seg_k = np.concatenate([np.arange(T) // 256, np.full(256, 8)])
    for s in range(9):
        km[s] = (seg_k == s)
        if prompt:
            qm[s] = BIG * ((seg_q == s) if s < 8 else 0.0)
        else:
            qm[s] = BIG
    km[9] = 1.0
    qm[9] = -BIG
    d['qmask'] = qm
    d['kmask'] = km
    if prompt:
        d['ckT'] = np.zeros((DEPTH, 4, 64, 256), f32)
        d['cv'] = np.zeros((DEPTH, 128, 2, 256), f32)
        d['h0'] = np.zeros((DEPTH, 4, 128, 16), f32)
    else:
        b = core - 4
        ck = inp['cache_k'][b]
        d['ckT'] = np.ascontiguousarray(ck.transpose(0, 2, 3, 1))
        cvv = inp['cache_v'][b]
        d['cv'] = np.ascontiguousarray(cvv.reshape(DEPTH, 2, 128, 256).transpose(0, 2, 1, 3))
        d['h0'] = _state_layout(inp['state_ssm_re'][b], inp['state_ssm_im'][b])
    return d


def _tile_layout(a):
    L = a.shape[0]
    rest = a.shape[4:]
    a = a.reshape((L, 2, 4, 8, 64) + rest)
    nd = a.ndim
    perm = (0, 2, 1, 4, 3) + tuple(range(5, nd))
    a = a.transpose(perm)
    return np.ascontiguousarray(a.reshape((L, 4, 128, 8) + rest))


def _state_layout(sre, sim):
    r = _tile_layout(sre)
    i = _tile_layout(sim)
    return np.ascontiguousarray(np.stack([r, i], axis=3).reshape(DEPTH, 4, 128, 16))


def _shared_inputs(inp):
    f32 = np.float32
    ident, rot, amask, jmat = _consts()
    d = {'ident': ident, 'rot': rot, 'amask': amask, 'jmat': jmat}
    d['w_ada'] = inp['w_ada']
    d['b_ada'] = np.ascontiguousarray(inp['b_ada'].reshape(DEPTH, 48, 128).transpose(0, 2, 1))
    d['n1g'] = np.ascontiguousarray(inp['norm1_g'].reshape(DEPTH, NT, 128).transpose(0, 2, 1))
    d['n2g'] = np.ascontiguousarray(inp['norm2_g'].reshape(DEPTH, NT, 128).transpose(0, 2, 1))
    d['fng'] = np.ascontiguousarray(inp['final_norm_g'].reshape(NT, 128).T)
    d['w_in'] = inp['w_in']
    cw = np.concatenate([inp['conv_w'], inp['conv_b'][:, None, :]], axis=1)
    d['convw'] = np.ascontiguousarray(cw.reshape(DEPTH, 4, 4, 128).transpose(0, 3, 2, 1))
    d['qkg'] = np.ascontiguousarray(np.stack([inp['q_norm_g'], inp['k_norm_g']], axis=2))
    lre = _tile_layout(inp['ssm_lambda_re'])
    lim = _tile_layout(inp['ssm_lambda_im'])
    ldt = _tile_layout(np.broadcast_to(inp['ssm_log_dt'][..., None], (DEPTH, 2, 32, 64)).copy())
    d['lam'] = np.ascontiguousarray(np.stack([lre, lim, ldt], axis=3))
    bre = _tile_layout(inp['ssm_b_re']).reshape(DEPTH, 4, 128, 128)
    bim = _tile_layout(inp['ssm_b_im']).reshape(DEPTH, 4, 128, 128)
    d['sB'] = np.ascontiguousarray(np.stack([bre, bim], axis=3))
    cre = _tile_layout(np.ascontiguousarray(inp['ssm_c_re'].transpose(0, 1, 2, 4, 3))).reshape(DEPTH, 4, 128, 128)
    cim = _tile_layout(np.ascontiguousarray(inp['ssm_c_im'].transpose(0, 1, 2, 4, 3))).reshape(DEPTH, 4, 128, 128)
    d['sC'] = np.ascontiguousarray(np.stack([cre, cim], axis=3))
    sd = inp['ssm_d'].reshape(DEPTH, 32, 16)
    d['sD'] = np.ascontiguousarray(np.broadcast_to(sd.transpose(0, 2, 1)[:, None], (DEPTH, 8, 16, 32)).reshape(DEPTH, 128, 32))
    d['w_glu'] = inp['w_glu']
    d['b_glu'] = np.ascontiguousarray(inp['b_glu'].reshape(DEPTH, 4, 128).transpose(0, 2, 1))
    d['wpc'] = inp['w_proj_conv']
    d['wpa'] = inp['w_proj_attn']
    d['wps'] = inp['w_proj_ssm']
    d['w_out'] = inp['w_out']
    d['wfg'] = inp['w_ffn_gate']
    d['wfu'] = inp['w_ffn_up']
    d['wfd'] = inp['w_ffn_down']
    return {k: np.ascontiguousarray(v, dtype=f32) for k, v in d.items()}


def kernel(**inputs):
    inp = {k: np.asarray(v) for k, v in inputs.items()}
    if 'nc' not in _PROG:
        _PROG['nc'], _PROG['es'] = build_program()
    nc = _PROG['nc']
    shared = _shared_inputs(inp)
    in_maps = [_core_inputs(c, inp, shared) for c in range(8)]
    res = run_bass_kernel_spmd(nc, in_maps, core_ids=list(range(8)))
    R = res.results
    f32 = np.float32
    y_prompt = np.concatenate([R[c]['yT'].T.reshape(8, 256, D) for c in range(4)], axis=0).astype(f32)
    y_sample = np.stack([R[c]['yT'].T for c in range(4, 8)], axis=0).astype(f32)
    nk = np.concatenate([R[c]['kout'].reshape(DEPTH, 4, 64, 8, 256).transpose(3, 0, 4, 1, 2) for c in range(4)], axis=0)
    nv = np.concatenate([R[c]['vout'].reshape(DEPTH, 4, 128, 16, 64).transpose(0, 3, 2, 1, 4).reshape(DEPTH, 8, 256, 4, 64)
                         .transpose(1, 0, 2, 3, 4) for c in range(4)], axis=0)
    sre, sim = [], []
    for c in range(4):
        s = R[c]['sout'].reshape(DEPTH, 4, 2, 64, 8, 2, 8)
        s = s.transpose(4, 0, 5, 2, 1, 6, 3)
        s = s.reshape(8, DEPTH, 2, 2, 32, 64).copy()
        s[:, :, :, 1] = s[::-1, :, :, 1]
        sre.append(s[:, :, 0])
        sim.append(s[:, :, 1])
    new_re = np.concatenate(sre, axis=0)
    new_im = np.concatenate(sim, axis=0)
    return (y_prompt, y_sample, np.ascontiguousarray(nk, dtype=f32), np.ascontiguousarray(nv, dtype=f32),
            np.ascontiguousarray(new_re, dtype=f32), np.ascontiguousarray(new_im, dtype=f32))
```
